# Optimizing a Trainium2 kernel written in Bass

```python
import math
import jax, jax.numpy as jnp
from jax import lax
import numpy as np

D_MODEL = 2048
BATCH = 2
SEQ = 8192
DEPTH = 1

N_MEM = 256
DA_HEAD_DIM = 64
DA_WIDTH = D_MODEL // 2
DA_HEADS = DA_WIDTH // (2 * DA_HEAD_DIM)
DA_ROT_DIM = DA_HEAD_DIM // 4
SB_HEAD_DIM = 128
SB_WIDTH = D_MODEL // 2
SB_HEADS = SB_WIDTH // SB_HEAD_DIM
XA_HEADS = 4
XA_HEAD_DIM = 128
XA_WIDTH = XA_HEADS * XA_HEAD_DIM
D_FF = 256 * ((8 * D_MODEL // 3 + 255) // 256)
CONV_WIDTH = 3
ROPE_THETA = 500000.0
Q_BLOCK = 128
EPS = 1e-6
IN_WIDTHS = (DA_WIDTH, DA_WIDTH, DA_WIDTH, SB_WIDTH, SB_WIDTH, SB_WIDTH, D_MODEL, D_MODEL)
N_IN = 3 * DA_WIDTH + 3 * SB_WIDTH + 2 * D_MODEL

kernel_name = "hybrid_diffattn_stickbreaking_gated_block"


def rms_norm(x, g):
    xf = x.astype(jnp.float32)
    y = xf * lax.rsqrt(jnp.mean(xf * xf, axis=-1, keepdims=True) + EPS)
    return (y * g.astype(jnp.float32)).astype(x.dtype)


def rope_tables(seq, rot_dim):
    inv = ROPE_THETA ** (-jnp.arange(0, rot_dim, 2, dtype=jnp.float32) / rot_dim)
    ang = jnp.arange(seq, dtype=jnp.float32)[:, None] * inv[None, :]
    return jnp.cos(ang), jnp.sin(ang)


def partial_rope(x, cos, sin):
    half = cos.shape[-1]
    shape = (1, x.shape[1]) + (1,) * (x.ndim - 3) + (half,)
    c = cos.reshape(shape).astype(x.dtype)
    s = sin.reshape(shape).astype(x.dtype)
    x1, x2, xp = x[..., :half], x[..., half:2 * half], x[..., 2 * half:]
    return jnp.concatenate([x1 * c - x2 * s, x1 * s + x2 * c, xp], axis=-1)


def split_columns(z, widths):
    outs, start = [], 0
    for w in widths:
        outs.append(z[..., start:start + w])
        start += w
    return outs


def differential_attention(q, k, v, lam):
    b, s, h, _, d = q.shape
    nb = s // Q_BLOCK
    scale = d ** -0.5
    qh = q.transpose(0, 2, 3, 1, 4)
    qblk = qh.reshape(b, h, 2, nb, Q_BLOCK, d).transpose(3, 0, 1, 2, 4, 5)
    kh = k.transpose(0, 2, 3, 1, 4)
    vh = v.transpose(0, 2, 1, 3)
    key_pos = jnp.arange(s)

    def block(args):
        qi, i = args
        q_pos = i * Q_BLOCK + jnp.arange(Q_BLOCK)
        sc = jnp.einsum('bhmqd,bhmkd->bhmqk', qi, kh).astype(jnp.float32) * scale
        causal = key_pos[None, :] <= q_pos[:, None]
        p = jax.nn.softmax(jnp.where(causal, sc, -jnp.inf), axis=-1)
        a = p[:, :, 0] - lam * p[:, :, 1]
        return jnp.einsum('bhqk,bhkv->bhqv', a.astype(vh.dtype), vh)

    o = lax.map(block, (qblk, jnp.arange(nb)))
    return o.transpose(1, 0, 3, 2, 4).reshape(b, s, h, 2 * d)


def stick_breaking_attention(q, k, v):
    b, s, h, d = q.shape
    nb = s // Q_BLOCK
    scale = d ** -0.5
    qh = q.transpose(0, 2, 1, 3)
    qblk = qh.reshape(b, h, nb, Q_BLOCK, d).transpose(2, 0, 1, 3, 4)
    kh = k.transpose(0, 2, 1, 3)
    vh = v.transpose(0, 2, 1, 3)
    key_pos = jnp.arange(s)

    def block(args):
        qi, i = args
        q_pos = i * Q_BLOCK + jnp.arange(Q_BLOCK)
        z = jnp.einsum('bhqd,bhkd->bhqk', qi, kh).astype(jnp.float32) * scale
        strict = key_pos[None, :] < q_pos[:, None]
        log_beta = jax.nn.log_sigmoid(z)
        log_1m_beta = jnp.where(strict, jax.nn.log_sigmoid(-z), 0.0)
        tail = lax.cumsum(log_1m_beta, axis=log_1m_beta.ndim - 1, reverse=True) - log_1m_beta
        a = jnp.where(strict, jnp.exp(log_beta + tail), 0.0)
        return jnp.einsum('bhqk,bhkd->bhqd', a.astype(vh.dtype), vh)

    o = lax.map(block, (qblk, jnp.arange(nb)))
    return o.transpose(1, 0, 3, 2, 4).reshape(b, s, h, d)


def parallel_mixer(h, w_in, lambda_q1, lambda_k1, lambda_q2, lambda_k2, da_subln_g,
                   w_proj_a, w_proj_b, w_out, cos, sin, lam_init):
    b, s, _ = h.shape
    z = h @ w_in
    qa, ka, va, qb, kb, vb, ga, gb = split_columns(z, IN_WIDTHS)
    qa = partial_rope(qa.reshape(b, s, DA_HEADS, 2, DA_HEAD_DIM), cos, sin)
    ka = partial_rope(ka.reshape(b, s, DA_HEADS, 2, DA_HEAD_DIM), cos, sin)
    va = va.reshape(b, s, DA_HEADS, 2 * DA_HEAD_DIM)
    lq1, lk1 = lambda_q1.astype(jnp.float32), lambda_k1.astype(jnp.float32)
    lq2, lk2 = lambda_q2.astype(jnp.float32), lambda_k2.astype(jnp.float32)
    lam = jnp.exp(jnp.sum(lq1 * lk1)) - jnp.exp(jnp.sum(lq2 * lk2)) + lam_init
    oa = differential_attention(qa, ka, va, lam)
    oa = (rms_norm(oa, da_subln_g) * (1.0 - lam_init)).reshape(b, s, DA_WIDTH)
    ob = stick_breaking_attention(qb.reshape(b, s, SB_HEADS, SB_HEAD_DIM),
                                  kb.reshape(b, s, SB_HEADS, SB_HEAD_DIM),
                                  vb.reshape(b, s, SB_HEADS, SB_HEAD_DIM)).reshape(b, s, SB_WIDTH)
    merged = jax.nn.sigmoid(ga) * (oa @ w_proj_a) + jax.nn.sigmoid(gb) * (ob @ w_proj_b)
    return merged @ w_out


def memory_cross_attention(h, mem_n, w_xq, w_xkv, w_xo):
    b, s, _ = h.shape
    m = mem_n.shape[1]
    q = (h @ w_xq).reshape(b, s, XA_HEADS, XA_HEAD_DIM)
    kv = mem_n @ w_xkv
    k = kv[..., :XA_WIDTH].reshape(b, m, XA_HEADS, XA_HEAD_DIM)
    v = kv[..., XA_WIDTH:].reshape(b, m, XA_HEADS, XA_HEAD_DIM)
    sc = jnp.einsum('bshd,bmhd->bhsm', q, k).astype(jnp.float32) * (XA_HEAD_DIM ** -0.5)
    p = jax.nn.softmax(sc, axis=-1)
    o = jnp.einsum('bhsm,bmhd->bshd', p.astype(v.dtype), v).reshape(b, s, XA_WIDTH)
    return o @ w_xo


def causal_depthwise_conv(u, w, bias):
    c = u.shape[-1]
    y = lax.conv_general_dilated(u, w[:, None, :].astype(u.dtype), window_strides=(1,),
                                 padding=((CONV_WIDTH - 1, 0),),
                                 dimension_numbers=('NWC', 'WIO', 'NWC'),
                                 feature_group_count=c)
    return y + bias.astype(u.dtype)


def conv_ffn(h, w_up, conv_w, conv_b, w_down):
    u = causal_depthwise_conv(h @ w_up, conv_w, conv_b)
    gate, val = u[..., :D_FF], u[..., D_FF:]
    return (jax.nn.silu(gate) * val) @ w_down


def setup_inputs(seed: int = 0) -> dict:
    key = jax.random.key(seed)
    ks = jax.random.split(key, 24)

    def nrm(k, shape, scale):
        return jax.random.normal(k, shape, dtype=jnp.float32) * scale

    def gain(k, shape):
        return 1.0 + nrm(k, shape, 0.01)

    L, D = DEPTH, D_MODEL
    return {
        "x": nrm(ks[0], (BATCH, SEQ, D), 1.0),
        "mem": nrm(ks[1], (BATCH, N_MEM, D), 1.0),
        "norm_mix_g": gain(ks[2], (L, D)),
        "w_in": nrm(ks[3], (L, D, N_IN), D ** -0.5),
        "lambda_q1": nrm(ks[4], (L, DA_HEAD_DIM), 0.1),
        "lambda_k1": nrm(ks[5], (L, DA_HEAD_DIM), 0.1),
        "lambda_q2": nrm(ks[6], (L, DA_HEAD_DIM), 0.1),
        "lambda_k2": nrm(ks[7], (L, DA_HEAD_DIM), 0.1),
        "da_subln_g": gain(ks[8], (L, 2 * DA_HEAD_DIM)),
        "w_proj_a": nrm(ks[9], (L, DA_WIDTH, D), DA_WIDTH ** -0.5),
        "w_proj_b": nrm(ks[10], (L, SB_WIDTH, D), SB_WIDTH ** -0.5),
        "w_out": nrm(ks[11], (L, D, D), D ** -0.5),
        "norm_x_g": gain(ks[12], (L, D)),
        "norm_mem_g": gain(ks[13], (L, D)),
        "w_xq": nrm(ks[14], (L, D, XA_WIDTH), D ** -0.5),
        "w_xkv": nrm(ks[15], (L, D, 2 * XA_WIDTH), D ** -0.5),
        "w_xo": nrm(ks[16], (L, XA_WIDTH, D), XA_WIDTH ** -0.5),
        "norm_ffn_g": gain(ks[17], (L, D)),
        "w_up": nrm(ks[18], (L, D, 2 * D_FF), D ** -0.5),
        "conv_w": nrm(ks[19], (L, CONV_WIDTH, 2 * D_FF), CONV_WIDTH ** -0.5),
        "conv_b": nrm(ks[20], (L, 2 * D_FF), 0.01),
        "w_down": nrm(ks[21], (L, D_FF, D), D_FF ** -0.5),
        "final_norm_g": gain(ks[22], (D,)),
    }


def reference(x, mem, norm_mix_g, w_in, lambda_q1, lambda_k1, lambda_q2, lambda_k2,
              da_subln_g, w_proj_a, w_proj_b, w_out, norm_x_g, norm_mem_g, w_xq, w_xkv,
              w_xo, norm_ffn_g, w_up, conv_w, conv_b, w_down, final_norm_g):
    cos, sin = rope_tables(x.shape[1], DA_ROT_DIM)
    for l in range(DEPTH):
        lam_init = 0.8 - 0.6 * math.exp(-0.3 * l)
        x = x + parallel_mixer(rms_norm(x, norm_mix_g[l]), w_in[l],
                               lambda_q1[l], lambda_k1[l], lambda_q2[l], lambda_k2[l],
                               da_subln_g[l], w_proj_a[l], w_proj_b[l], w_out[l],
                               cos, sin, lam_init)
        x = x + memory_cross_attention(rms_norm(x, norm_x_g[l]), rms_norm(mem, norm_mem_g[l]),
                                       w_xq[l], w_xkv[l], w_xo[l])
        x = x + conv_ffn(rms_norm(x, norm_ffn_g[l]), w_up[l], conv_w[l], conv_b[l], w_down[l])
    return rms_norm(x, final_norm_g)
```

```python
import math
from contextlib import ExitStack

import numpy as np
import concourse.bass as bass
import concourse.mybir as mybir
from concourse.bass_utils import run_bass_kernel_spmd

F32 = mybir.dt.float32
BF16 = mybir.dt.bfloat16
AF = mybir.ActivationFunctionType
ALU = mybir.AluOpType

D = 2048
S = 8192
KC = 16
NMEM = 256
DFF = 5632
NFC = 44
HALO = 16
WT = 512
NTILE = 4
EPS = 1e-6
LAM_INIT = 0.8 - 0.6 * math.exp(0.0)
SEM_ROT = 12000
NATT = HALO + S


class Buf:
    __slots__ = ("name", "w", "r", "multi", "excl")

    def __init__(self, name, multi=False, excl=False):
        self.name = name
        self.w = []
        self.r = []
        self.multi = multi
        self.excl = excl


class Eng:
    def __init__(self, prog, name):
        self.prog = prog
        self.name = name
        self.q = []
        self.sem = None
        self.cnt = 0
        self.waited = {}

    def newsem(self):
        self.sem = self.prog.new_sem(self.name)
        self.cnt = 0


class Prog:
    def __init__(self, nc, stack):
        self.nc = nc
        self.stack = stack
        self.nsem = 0
        self.dsem = {}
        self.sem_owner = {}
        self.engs = {n: Eng(self, n) for n in ("tensor", "scalar", "vector", "gpsimd", "sync")}
        for e in self.engs.values():
            e.newsem()
        self.ninst = 0

    def new_sem(self, name):
        self.nsem += 1
        sem = self.stack.enter_context(self.nc.semaphore(f"s{self.nsem}{name[:2]}"))
        self.sem_owner[id(sem)] = name
        return sem

    def _waits(self, eng, toks):
        need = {}
        for (sem, val) in toks:
            k = id(sem)
            if eng.waited.get(k, 0) >= val:
                continue
            if k not in need or need[k][1] < val:
                need[k] = (sem, val)
        for k, (sem, val) in need.items():
            eng.waited[k] = val
            eng.q.append(lambda e, sem=sem, val=val: e.wait_ge(sem, val))
            self.ninst += 1

    def _deps(self, reads, writes, extra, engname=None):
        toks = list(extra)
        for b in reads:
            toks += b.w
            if b.excl:
                toks += [t for t in b.r if self.sem_owner.get(id(t[0])) != engname]
        for b in writes:
            if not b.multi:
                toks += b.w
                toks += b.r
        return toks

    def _commit(self, tok, reads, writes):
        for b in writes:
            if b.multi:
                b.w = [t for t in b.w if t[0] is not tok[0]] + [tok]
            else:
                b.w = [tok]
                b.r = []
        for b in reads:
            if b not in writes:
                b.r = [t for t in b.r if t[0] is not tok[0]] + [tok]

    def op(self, engname, fns, reads=(), writes=(), extra=()):
        if not isinstance(fns, (list, tuple)):
            fns = [fns]
        eng = self.engs[engname]
        self._waits(eng, self._deps(reads, writes, extra, engname))
        for fn in fns[:-1]:
            eng.q.append(lambda e, fn=fn: fn(e))
        if eng.cnt >= SEM_ROT:
            eng.newsem()
        eng.cnt += 1
        sem = eng.sem
        eng.q.append(lambda e, fn=fns[-1], sem=sem: fn(e).then_inc(sem, 1))
        self.ninst += len(fns)
        tok = (sem, eng.cnt)
        self._commit(tok, reads, writes)
        return tok

    def dma(self, engname, fns, owner, reads=(), writes=(), extra=()):
        if not isinstance(fns, (list, tuple)):
            fns = [fns]
        eng = self.engs[engname]
        self._waits(eng, self._deps(reads, writes, extra, engname))
        st = self.dsem.setdefault(id(owner), [None, 0, owner])
        if st[0] is None or st[1] + 16 * len(fns) > 16 * 1200:
            st[0] = self.new_sem("d")
            st[1] = 0
        sem = st[0]
        for fn in fns:
            eng.q.append(lambda e, fn=fn, sem=sem: fn(e).then_inc(sem, 16))
        self.ninst += len(fns)
        st[1] += 16 * len(fns)
        tok = (sem, st[1])
        self._commit(tok, reads, writes)
        return tok

    def all_tokens(self):
        toks = [(e.sem, e.cnt) for e in self.engs.values() if e.cnt > 0]
        toks += [(st[0], st[1]) for st in self.dsem.values() if st[0] is not None and st[1] > 0]
        return toks

    def barrier(self):
        toks = self.all_tokens()
        for e in self.engs.values():
            self._waits(e, toks)

    def finish(self, block):
        self.barrier()
        engs = self.engs

        @block.tensor
        def _(e):
            for fn in engs["tensor"].q:
                fn(e)

        @block.scalar
        def _(e):
            for fn in engs["scalar"].q:
                fn(e)

        @block.vector
        def _(e):
            for fn in engs["vector"].q:
                fn(e)

        @block.gpsimd
        def _(e):
            for fn in engs["gpsimd"].q:
                fn(e)

        @block.sync
        def _(e):
            for fn in engs["sync"].q:
                fn(e)


class T:
    def __init__(self, ap_tensor, name, excl=False):
        self.t = ap_tensor
        self.b = Buf(name, excl=excl)

    def __getitem__(self, idx):
        return self.t[idx]


def build(stop_after=None, debug=False):
    nc = bass.Bass("TRN2", target_bir_lowering=False)

    def din(name, shape):
        return nc.dram_tensor(name, list(shape), F32, kind="ExternalInput")

    xb = din("xb", [S, D])
    xo = din("xo", [HALO + NTILE * WT, D])
    memb = din("memb", [NMEM, D])
    wqk = din("wqk", [8, 128, KC, 128])
    wv = din("wv", [128, KC, 512])
    wg = din("wg", [32, 128, KC, 128])
    wpa = din("wpa", [16, 128, 8, 128])
    wpb = din("wpb", [16, 128, 8, 128])
    wout = din("wout", [16, 128, KC, 128])
    wxq = din("wxq", [4, 128, KC, 128])
    wxk = din("wxk", [4, 128, KC, 128])
    wxv = din("wxv", [128, KC, 512])
    wxo = din("wxo", [16, 128, 4, 128])
    wup = din("wup", [2 * NFC, 128, KC, 128])
    wdn = din("wdn", [16, 128, NFC, 128])
    gains = din("gains", [128, 5 * KC + 1])
    convw = din("convw", [128, 2 * NFC * 3])
    convb = din("convb", [128, 2 * NFC])
    lam4 = din("lam4", [128, 4 * 64])
    ropeC = din("ropeC", [128, S])
    ropeS = din("ropeS", [128, S])
    consts = din("consts", [128, 5 * 128])
    masks = din("masks", [128, 8])
    y = nc.dram_tensor("y", [NTILE * WT, D], F32, kind="ExternalOutput")

    qkT = nc.dram_tensor("qkT_s", [8, 128, S], BF16)
    vS = nc.dram_tensor("v_s", [4, 128, 64, 128], BF16)
    att_send = [nc.dram_tensor(f"att_send{k}", [512, 512], F32) for k in range(8)]
    att_all = [nc.dram_tensor(f"att_all{k}", [4 * 512, 512], F32) for k in range(8)]
    att_send_bf = [t[:, :].bitcast(BF16) for t in att_send]
    att_all_bf = [t[:, :].bitcast(BF16) for t in att_all]

    dbg = {}
    if debug:
        dbg["qk"] = nc.dram_tensor("dbg_qk", [8, 128, S], BF16, kind="ExternalOutput")
        dbg["v"] = nc.dram_tensor("dbg_v", [4, 128, 64, 128], BF16, kind="ExternalOutput")
        dbg["att"] = nc.dram_tensor("dbg_att", [512, 512], F32, kind="ExternalOutput")
        dbg["x"] = nc.dram_tensor("dbg_x", [5, 128, KC * WT], F32, kind="ExternalOutput")

    root = ExitStack()
    with root:
        P = Prog(nc, root)

        uid = [0]

        def sb(stack, name, shape, dt):
            uid[0] += 1
            return T(stack.enter_context(nc.sbuf_tensor(f"{name}_{uid[0]}", list(shape), dt)), name)

        ps_all = root.enter_context(nc.psum_tensor("ps_all", [128, 8, 512], F32))
        psb = [T(ps_all[:, i, :], f"ps{i}", excl=True) for i in range(8)]
        ps_rr = [0]

        def pbank():
            ps_rr[0] = (ps_rr[0] + 1) % 8
            return psb[ps_rr[0]]

        d_qk = Buf("d_qk", multi=True)
        d_v = Buf("d_v", multi=True)
        d_send = [Buf(f"d_send{k}", multi=True) for k in range(8)]
        d_all = [Buf(f"d_all{k}", multi=True) for k in range(8)]
        d_y = Buf("d_y", multi=True)
        d_dbg = Buf("d_dbg", multi=True)

        c32 = sb(root, "c32", [128, 5 * 128], F32)
        cbf = sb(root, "cbf", [128, 5 * 128], BF16)
        gn = sb(root, "gn", [128, 5 * KC + 1], F32)
        cw = sb(root, "cw", [128, 2 * NFC * 3], F32)
        cb = sb(root, "cb", [128, 2 * NFC], F32)
        mk = sb(root, "mk", [128, 8], F32)
        lm = sb(root, "lm", [128, 256], F32)
        lmp = sb(root, "lmp", [128, 128], F32)
        lms = sb(root, "lms", [128, 8], F32)
        P.dma("sync", lambda e: e.dma_start(out=c32[:, :], in_=consts[:, :]), c32.b, writes=[c32.b])
        P.dma("sync", lambda e: e.dma_start(out=gn[:, :], in_=gains[:, :]), gn.b, writes=[gn.b])
        P.dma("sync", lambda e: e.dma_start(out=cw[:, :], in_=convw[:, :]), cw.b, writes=[cw.b])
        P.dma("sync", lambda e: e.dma_start(out=cb[:, :], in_=convb[:, :]), cb.b, writes=[cb.b])
        P.dma("sync", lambda e: e.dma_start(out=mk[:, :], in_=masks[:, :]), mk.b, writes=[mk.b])
        P.dma("sync", lambda e: e.dma_start(out=lm[:, :], in_=lam4[:, :]), lm.b, writes=[lm.b])
        P.op("vector", lambda e: e.tensor_copy(cbf[:, :], c32[:, :]), reads=[c32.b], writes=[cbf.b])
        ident32 = c32[:, 0:128]
        perm_bf = cbf[:, 128:256]
        tinc_bf = cbf[:, 256:384]
        tcomp_bf = cbf[:, 384:512]
        ones_bf = cbf[:, 512:640]
        P.op("vector", lambda e: e.tensor_tensor(lmp[:, 0:64], lm[:, 0:64], lm[:, 64:128], ALU.mult),
             reads=[lm.b], writes=[lmp.b])
        P.op("vector", lambda e: e.tensor_tensor(lmp[:, 64:128], lm[:, 128:192], lm[:, 192:256], ALU.mult),
             reads=[lm.b], writes=[lmp.b])
        P.op("vector", lambda e: e.tensor_reduce(lms[:, 0:1], lmp[:, 0:64], mybir.AxisListType.X, ALU.add),
             reads=[lmp.b], writes=[lms.b])
        P.op("vector", lambda e: e.tensor_reduce(lms[:, 1:2], lmp[:, 64:128], mybir.AxisListType.X, ALU.add),
             reads=[lmp.b], writes=[lms.b])
        P.op("scalar", lambda e: e.activation(lms[:, 2:4], lms[:, 0:2], AF.Exp), reads=[lms.b], writes=[lms.b])
        P.op("vector", lambda e: e.scalar_tensor_tensor(lms[:, 5:6], lms[:, 3:4], -LAM_INIT, lms[:, 2:3],
                                                         ALU.add, ALU.subtract),
             reads=[lms.b], writes=[lms.b])
        P.op("vector", lambda e: e.tensor_scalar(lms[:, 6:7], gn[:, 5 * KC:5 * KC + 1], 1.0 - LAM_INIT, None,
                                                  ALU.mult),
             reads=[gn.b, lms.b], writes=[lms.b])
        neglam = lms[:, 5:6]
        agmark = sb(root, "agmark", [128, 8], F32)
        onec = sb(root, "onec", [128, 1], F32)
        P.op("vector", lambda e: e.memset(onec[:, :], 1.0), writes=[onec.b])
        epsc = sb(root, "epsc", [128, 1], F32)
        P.op("vector", lambda e: e.memset(epsc[:, :], EPS), writes=[epsc.b])
        gsub8 = lms[:, 6:7]

        def load_T(stack_tmp, src_rows_ap_fn, nsub, sizes, xT, tag=None):
            for s in range(nsub):
                n = sizes[s]
                xs = stack_tmp[s % len(stack_tmp)]
                P.dma("sync", lambda e, xs=xs, s=s, n=n: e.dma_start(out=xs[0:n, :], in_=src_rows_ap_fn(s, n)),
                      xs.b, writes=[xs.b])
                for g in range(4):
                    pb = pbank()
                    fns = []
                    for j in range(4):
                        kc = 4 * g + j
                        fns.append(lambda e, pb=pb, xs=xs, kc=kc, j=j, n=n: e.transpose(
                            pb[:, j * 128:j * 128 + n], xs[0:n, kc * 128:(kc + 1) * 128], ident32[0:n, 0:n]))
                    P.op("tensor", fns, reads=[xs.b, c32.b], writes=[pb.b])
                    off = sum(sizes[:s])
                    eng = "vector" if g % 2 == 0 else "scalar"
                    if eng == "vector":
                        P.op("vector", lambda e, pb=pb, g=g, off=off, n=n: e.tensor_copy(
                            xT[:, 4 * g:4 * g + 4, off:off + n],
                            pb[:, :].rearrange("p (j t) -> p j t", j=4)[:, :, 0:n]),
                            reads=[pb.b], writes=[xT.b])
                    else:
                        P.op("scalar", lambda e, pb=pb, g=g, off=off, n=n: e.activation(
                            xT[:, 4 * g:4 * g + 4, off:off + n],
                            pb[:, :].rearrange("p (j t) -> p j t", j=4)[:, :, 0:n], AF.Copy),
                            reads=[pb.b], writes=[xT.b])

        def rmsnorm_T(xT, W, gcol0, outT, sq, rstd, nfeat=D):
            nk = nfeat // 128
            P.op("scalar", lambda e: e.activation(sq[:, 0:nk, 0:W], xT[:, 0:nk, 0:W], AF.Square),
                 reads=[xT.b], writes=[sq.b])
            pb = pbank()
            fns = [lambda e, kc=kc: e.matmul(pb[:, 0:W], ones_bf, sq[:, kc, 0:W], start=(kc == 0), stop=(kc == nk - 1))
                   for kc in range(nk)]
            P.op("tensor", fns, reads=[sq.b, cbf.b], writes=[pb.b])
            P.op("scalar", lambda e: e.activation(rstd[:, 0:W], pb[:, 0:W], AF.Sqrt, bias=epsc[:, 0:1], scale=1.0 / nfeat),
                 reads=[pb.b, epsc.b], writes=[rstd.b])
            P.op("vector", lambda e: e.reciprocal(rstd[:, 0:W], rstd[:, 0:W]),
                 reads=[rstd.b], writes=[rstd.b])
            for kc in range(nk):
                P.op("vector", lambda e, kc=kc: e.scalar_tensor_tensor(
                    outT[:, kc, 0:W], xT[:, kc, 0:W], gn[:, gcol0 + kc:gcol0 + kc + 1], rstd[:, 0:W],
                    ALU.mult, ALU.mult), reads=[xT.b, rstd.b, gn.b], writes=[outT.b])

        wq_rr = [0]

        def load_w(slots, dram_ap, shape_free):
            wq_rr[0] += 1
            slot = slots[wq_rr[0] % len(slots)]
            n = 1
            for s_ in shape_free:
                n *= s_
            P.dma("gpsimd", lambda e, slot=slot: e.dma_start(out=slot[:, 0:n], in_=dram_ap), slot.b, writes=[slot.b])
            return slot

        with ExitStack() as ph:
            if stop_after == "c0":
                raise_skip = True
            wqk_sb = [sb(ph, f"wqk{c}", [128, KC * 128], BF16) for c in range(8)]
            wv_sb = sb(ph, "wv_sb", [128, KC * 512], BF16)
            for c in range(8):
                P.dma("gpsimd", lambda e, c=c: e.dma_start(
                    out=wqk_sb[c][:, :], in_=wqk[c].rearrange("p k j -> p (k j)")), wqk_sb[c].b, writes=[wqk_sb[c].b])
            P.dma("gpsimd", lambda e: e.dma_start(out=wv_sb[:, :], in_=wv[:, :, :].rearrange("p k j -> p (k j)")),
                  wv_sb.b, writes=[wv_sb.b])
            xs_st = [sb(ph, f"xs{i}", [128, D], F32) for i in range(3)]
            xT = sb(ph, "p1xT", [128, KC, WT], F32)
            sq = sb(ph, "p1sq", [128, KC, WT], BF16)
            hT = sb(ph, "p1hT", [128, KC, WT], BF16)
            rstd = sb(ph, "p1rstd", [128, WT], F32)
            rc = [sb(ph, f"ropec{i}", [128, WT], F32) for i in range(2)]
            rs = [sb(ph, f"ropes{i}", [128, WT], F32) for i in range(2)]
            qraw = [sb(ph, f"qraw{i}", [128, WT], BF16) for i in range(2)]
            t1 = [sb(ph, f"rt1{i}", [128, WT], F32) for i in range(2)]
            t2 = [sb(ph, f"rt2{i}", [128, WT], F32) for i in range(2)]
            qo = [sb(ph, f"qo{i}", [128, WT], BF16) for i in range(4)]
            vo = [sb(ph, f"vo{i}", [128, 512], BF16) for i in range(3)]
            n_t1 = 0 if stop_after in ("c0", "c1") else (2 if (stop_after and "short" in stop_after) else S // WT)
            cnt = 0
            parts = stop_after.split(':')[1] if (stop_after and ':' in stop_after) else 'nqv123'
            if stop_after and ':' in stop_after:
                n_t1 = 2
            for i in range(n_t1):
                load_T(xs_st, lambda s, n, i=i: xb[(i * 4 + s) * 128:(i * 4 + s) * 128 + n, :], 4, [128] * 4, xT, "p1")
                if 'n' in parts:
                    rmsnorm_T(xT, WT, 0, hT, sq, rstd)
                rci, rsi = rc[i % 2], rs[i % 2]
                P.dma("sync", lambda e, rci=rci, i=i: e.dma_start(out=rci[:, :], in_=ropeC[:, i * WT:(i + 1) * WT]),
                      rci.b, writes=[rci.b])
                P.dma("sync", lambda e, rsi=rsi, i=i: e.dma_start(out=rsi[:, :], in_=ropeS[:, i * WT:(i + 1) * WT]),
                      rsi.b, writes=[rsi.b])
                for c in ([c_ for c_ in range(8) if ('q' in parts or ('a' in parts and c_ < 4) or ('b' in parts and c_ >= 4))]):
                    pb = pbank()
                    fns = [lambda e, kc=kc, c=c, pb=pb: e.matmul(
                        pb[:, :], wqk_sb[c][:, kc * 128:(kc + 1) * 128], hT[:, kc, :],
                        start=(kc == 0), stop=(kc == KC - 1)) for kc in range(KC)]
                    P.op("tensor", fns, reads=[wqk_sb[c].b, hT.b], writes=[pb.b])
                    cnt += 1
                    q_o = qo[cnt % 4]
                    if c < 4:
                        qr, a1, a2 = qraw[cnt % 2], t1[cnt % 2], t2[cnt % 2]
                        P.op("scalar", lambda e, qr=qr, pb=pb: e.activation(qr[:, :], pb[:, :], AF.Copy),
                             reads=[pb.b], writes=[qr.b])
                        pb2 = pbank()
                        if 'P' not in parts:
                            P.op("tensor", lambda e, pb2=pb2, qr=qr: e.matmul(pb2[:, :], perm_bf, qr[:, :], start=True, stop=True),
                                 reads=[qr.b, cbf.b], writes=[pb2.b])
                        if '1' in parts: P.op("vector", lambda e, a1=a1, pb=pb, rci=rci: e.scalar_tensor_tensor(a1[:, :], pb[:, :], 1.0, (rstd if 'R' in parts else rci)[:, :], ALU.mult, ALU.mult),
                             reads=[pb.b, (rstd if 'R' in parts else rci).b], writes=[a1.b])
                        if '2' in parts: P.op("vector", lambda e, a2=a2, pb2=pb2, rsi=rsi: e.scalar_tensor_tensor(a2[:, :], pb2[:, :], 1.0, rsi[:, :], ALU.mult, ALU.mult),
                             reads=[pb2.b, rsi.b], writes=[a2.b])
                        if '3' in parts: P.op("vector", lambda e, a1=a1, a2=a2, q_o=q_o: e.tensor_tensor(q_o[:, :], a1[:, :], a2[:, :], ALU.add),
                             reads=[a1.b, a2.b], writes=[q_o.b])
                    else:
                        sc = (128.0 ** -0.5) if c in (4, 6) else 1.0
                        P.op("scalar", lambda e, q_o=q_o, pb=pb, sc=sc: e.activation(q_o[:, :], pb[:, :], AF.Copy, scale=sc),
                             reads=[pb.b], writes=[q_o.b])
                    P.dma("sync", lambda e, q_o=q_o, c=c, i=i: e.dma_start(out=qkT[c, :, i * WT:(i + 1) * WT], in_=q_o[:, :]),
                          q_o.b, reads=[q_o.b], writes=[d_qk])
                for s in (range(4) if 'v' in parts else []):
                    pb = pbank()
                    fns = [lambda e, kc=kc, s=s, pb=pb: e.matmul(
                        pb[:, :], hT[:, kc, s * 128:(s + 1) * 128], wv_sb[:, kc * 512:(kc + 1) * 512],
                        start=(kc == 0), stop=(kc == KC - 1)) for kc in range(KC)]
                    P.op("tensor", fns, reads=[wv_sb.b, hT.b], writes=[pb.b])
                    cnt += 1
                    v_o = vo[cnt % 3]
                    P.op("scalar", lambda e, v_o=v_o, pb=pb: e.activation(v_o[:, :], pb[:, :], AF.Copy),
                         reads=[pb.b], writes=[v_o.b])
                    blk = i * 4 + s
                    P.dma("sync", lambda e, v_o=v_o, blk=blk: e.dma_start(
                        out=vS[:, :, blk, :].rearrange("h p d -> p h d"),
                        in_=v_o[:, :].rearrange("p (h d) -> p h d", h=4)),
                        v_o.b, reads=[v_o.b], writes=[d_v])
            P.barrier()

        short = bool(stop_after) and "short" in stop_after
        nqt = 2 if short else S // WT
        run_p2 = not (stop_after and (stop_after in ("c0", "c1", "p1", "p1short") or ":" in stop_after))
        ag_cnt = [0]
        if stop_after and stop_after.endswith("X"):
            for _ in range(20):
                P.new_sem("x")
        if run_p2:
          with ExitStack() as ph:
            KT = [sb(ph, f"KT{h}", [128, S], BF16) for h in range(2)]
            QT = [sb(ph, f"QT{h}", [128, S], BF16) for h in range(2)]
            Vh = [sb(ph, f"Vh{h}", [128, 64 * 128], BF16) for h in range(2)]

            sem_cc = P.new_sem("cc")

            def emit_ag(k):
                gp = P.engs["gpsimd"]
                P._waits(gp, P._deps([d_send[k]], [], []))
                gp.q.append(lambda e: e.collective_compute(
                    "AllGather", ALU.bypass, replica_groups=[[0, 1, 2, 3], [4, 5, 6, 7]],
                    ins=[att_send[k][:, :]], outs=[att_all[k][:, :]]).then_inc(sem_cc))
                ag_cnt[0] += 1
                d_all[k].w = [(sem_cc, ag_cnt[0])]

            def load_head(slot, cq, ck, cv):
                P.dma("sync", lambda e: e.dma_start(out=KT[slot][:, :], in_=qkT[ck]), KT[slot].b, reads=[d_qk], writes=[KT[slot].b])
                P.dma("sync", lambda e: e.dma_start(out=QT[slot][:, :], in_=qkT[cq]), QT[slot].b, reads=[d_qk], writes=[QT[slot].b])
                P.dma("sync", lambda e: e.dma_start(out=Vh[slot][:, :], in_=vS[cv].rearrange("p b d -> p (b d)")),
                      Vh[slot].b, reads=[d_v], writes=[Vh[slot].b])

            Eb = [sb(ph, f"Eb{i}", [128, 2, WT], BF16) for i in range(4)]
            accD = sb(ph, "accD", [128, 2, WT], F32)
            acc_hi = sb(ph, "acc_hi", [128, 2, WT], BF16)
            acc_lo = sb(ph, "acc_lo", [128, 2, WT], BF16)
            fin = [sb(ph, f"fin{i}", [128, WT], F32) for i in range(6)]
            sq1 = sb(ph, "sq1", [128, WT], BF16)
            onb = [sb(ph, f"onb{i}", [128, WT], BF16) for i in range(2)]
            for hA in range(2):
                load_head(hA, 2 * hA, 2 * hA + 1, hA)
            Dn0, Dn1, O0, O1 = psb[4], psb[5], psb[6], psb[7]
            for hA in range(2):
                K_, Q_, V_ = KT[hA], QT[hA], Vh[hA]
                units = [(i, j) for i in range(nqt) for j in range(4 * i + 4)]

                def emit_S(u, K_=K_, Q_=Q_):
                    i, j = units[u]
                    jj = j - 4 * i
                    n0 = 128 * jj if jj > 0 else 0
                    q0 = i * WT
                    b0, b1 = psb[2 * (u % 2)], psb[2 * (u % 2) + 1]
                    P.op("tensor", [
                        lambda e: e.matmul(b0[:, n0:], K_[0:64, j * 128:(j + 1) * 128], Q_[0:64, q0 + n0:q0 + WT], start=True, stop=True),
                        lambda e: e.matmul(b1[:, n0:], K_[64:128, j * 128:(j + 1) * 128], Q_[64:128, q0 + n0:q0 + WT], start=True, stop=True)],
                        reads=[K_.b, Q_.b], writes=[b0.b, b1.b])
                    E = Eb[u % 4]
                    pair = ps_all[:, 2 * (u % 2):2 * (u % 2) + 2, n0:]
                    P.op("scalar", lambda e: e.activation(E[:, :, n0:], pair, AF.Exp, scale=0.125),
                         reads=[b0.b, b1.b], writes=[E.b])
                    if jj >= 0:
                        P.op("gpsimd", lambda e: e.affine_select(
                            out=E[:, :, n0:n0 + 128], in_=E[:, :, n0:n0 + 128], pattern=[[0, 2], [1, 128]],
                            compare_op=ALU.is_ge, fill=0.0, base=0, channel_multiplier=-1),
                            reads=[E.b], writes=[E.b])

                emit_S(0)

                def da_step(u, hA=hA, V_=V_):
                    i, j = units[u]
                    nj = 4 * i + 4
                    jj = j - 4 * i
                    n0 = 128 * jj if jj > 0 else 0
                    q0 = i * WT
                    if u + 1 < len(units):
                        emit_S(u + 1)
                    E = Eb[u % 4]
                    st, sp_ = (j == 0), (j == nj - 1)
                    P.op("tensor", [
                        lambda e, E=E, n0=n0, st=st, sp_=sp_, j=j: e.matmul(O0[:, n0:], V_[:, j * 128:(j + 1) * 128], E[:, 0, n0:], start=st, stop=sp_),
                        lambda e, E=E, n0=n0, st=st, sp_=sp_, j=j: e.matmul(O1[:, n0:], V_[:, j * 128:(j + 1) * 128], E[:, 1, n0:], start=st, stop=sp_)],
                        reads=[E.b, V_.b], writes=[O0.b, O1.b])
                    if st:
                        P.op("vector", lambda e, E=E: e.tensor_copy(accD[:, :, :], E[:, :, :]), reads=[E.b], writes=[accD.b])
                    else:
                        P.op("vector", lambda e, E=E, n0=n0: e.tensor_tensor(accD[:, :, n0:], accD[:, :, n0:], E[:, :, n0:], ALU.add),
                             reads=[E.b], writes=[accD.b])
                    if j == nj - 1:
                        r1, a1, r2, a2, o_, rs_ = fin
                        on = onb[i % 2]
                        P.op("vector", lambda e: e.tensor_copy(acc_hi[:, :, :], accD[:, :, :]), reads=[accD.b], writes=[acc_hi.b])
                        P.op("vector", lambda e: e.tensor_tensor(acc_lo[:, :, :], accD[:, :, :], acc_hi[:, :, :], ALU.subtract),
                             reads=[accD.b, acc_hi.b], writes=[acc_lo.b])
                        P.op("tensor", [
                            lambda e: e.matmul(Dn0[:, :], ones_bf, acc_hi[:, 0, :], start=True, stop=False),
                            lambda e: e.matmul(Dn0[:, :], ones_bf, acc_lo[:, 0, :], start=False, stop=True),
                            lambda e: e.matmul(Dn1[:, :], ones_bf, acc_hi[:, 1, :], start=True, stop=False),
                            lambda e: e.matmul(Dn1[:, :], ones_bf, acc_lo[:, 1, :], start=False, stop=True)],
                            reads=[acc_hi.b, acc_lo.b, cbf.b], writes=[Dn0.b, Dn1.b])
                        P.op("vector", lambda e: e.reciprocal(r1[:, :], Dn0[:, :]), reads=[Dn0.b], writes=[r1.b])
                        P.op("vector", lambda e: e.tensor_tensor(a1[:, :], O0[:, :], r1[:, :], ALU.mult), reads=[O0.b, r1.b], writes=[a1.b])
                        P.op("vector", lambda e: e.reciprocal(r2[:, :], Dn1[:, :]), reads=[Dn1.b], writes=[r2.b])
                        P.op("vector", lambda e: e.tensor_tensor(a2[:, :], O1[:, :], r2[:, :], ALU.mult), reads=[O1.b, r2.b], writes=[a2.b])
                        P.op("vector", lambda e: e.scalar_tensor_tensor(o_[:, :], a2[:, :], neglam, a1[:, :], ALU.mult, ALU.add),
                             reads=[a1.b, a2.b, lms.b], writes=[o_.b])
                        P.op("scalar", lambda e: e.activation(sq1[:, :], o_[:, :], AF.Square), reads=[o_.b], writes=[sq1.b])
                        pbk = psb[2 * ((u + 1) % 2)]
                        P.op("tensor", lambda e, pbk=pbk: e.matmul(pbk[:, :], ones_bf, sq1[:, :], start=True, stop=True),
                             reads=[sq1.b, cbf.b], writes=[pbk.b])
                        P.op("scalar", lambda e, pbk=pbk: e.activation(rs_[:, :], pbk[:, :], AF.Sqrt, bias=epsc[:, 0:1], scale=1.0 / 128),
                             reads=[pbk.b, epsc.b], writes=[rs_.b])
                        P.op("vector", lambda e: e.reciprocal(rs_[:, :], rs_[:, :]), reads=[rs_.b], writes=[rs_.b])
                        P.op("vector", lambda e, on=on: e.scalar_tensor_tensor(on[:, :], o_[:, :], gsub8, rs_[:, :], ALU.mult, ALU.mult),
                             reads=[o_.b, rs_.b, lms.b], writes=[on.b])
                        P.dma("sync", lambda e, on=on, hA=hA, i=i: e.dma_start(
                            out=att_send_bf[i // 2][hA * 128:(hA + 1) * 128, (i % 2) * WT:(i % 2 + 1) * WT], in_=on[:, :]),
                            on.b, reads=[on.b], writes=[d_send[i // 2]])

                for u in range(len(units)):
                    da_step(u)

            e32 = [sb(ph, f"e32_{i}", [128, 2, WT], F32) for i in range(3)]
            spb = [sb(ph, f"spb_{i}", [128, 2, WT], BF16) for i in range(3)]
            exb = [sb(ph, f"exb_{i}", [128, 2, WT], F32) for i in range(3)]
            abf = [sb(ph, f"abf_{i}", [128, 2, WT], BF16) for i in range(3)]
            obf = sb(ph, "obf", [128, 2, WT], BF16)
            zer = sb(ph, "zer", [128, 128], BF16)
            P.op("vector", lambda e: e.memset(zer[:, :], 0.0), writes=[zer.b])
            for h in range(2):
                load_head(h, 4 + 2 * h, 5 + 2 * h, 2 + h)
            units = [(i, j) for i in range(nqt) for j in reversed(range(4 * i + 4))]

            def geom(u):
                i, j = units[u]
                jj = j - 4 * i
                n0 = 128 * jj if jj > 0 else 0
                return i, j, jj, n0, i * WT

            def emit_Z(u):
                i, j, jj, n0, q0 = geom(u)
                k = u % 2
                P.op("tensor", [lambda e, h=h: e.matmul(psb[2 * k + h][:, n0:], KT[h][:, j * 128:(j + 1) * 128],
                                                        QT[h][:, q0 + n0:q0 + WT], start=True, stop=True) for h in range(2)],
                     reads=[KT[0].b, QT[0].b, KT[1].b, QT[1].b], writes=[psb[2 * k].b, psb[2 * k + 1].b])

            def emit_exp(u):
                i, j, jj, n0, q0 = geom(u)
                k = u % 2
                e_ = e32[u % 3]
                P.op("scalar", lambda e: e.activation(e_[:, :, n0:], ps_all[:, 2 * k:2 * k + 2, n0:], AF.Exp),
                     reads=[psb[2 * k].b, psb[2 * k + 1].b], writes=[e_.b])

            def emit_ln(u):
                i, j, jj, n0, q0 = geom(u)
                e_, s_ = e32[u % 3], spb[u % 3]
                P.op("scalar", lambda e: e.activation(s_[:, :, n0:], e_[:, :, n0:], AF.Ln, bias=onec[:, 0:1]),
                     reads=[e_.b, onec.b], writes=[s_.b])
                if jj >= 0:
                    for t_ in (s_, e_):
                        P.op("gpsimd", lambda e, t_=t_: e.affine_select(
                            out=t_[:, :, n0:n0 + 128], in_=t_[:, :, n0:n0 + 128], pattern=[[0, 2], [1, 128]],
                            compare_op=ALU.is_ge, fill=0.0, base=-1, channel_multiplier=-1),
                            reads=[t_.b], writes=[t_.b])

            emit_Z(0)
            emit_exp(0)
            emit_ln(0)
            if len(units) > 1:
                emit_Z(1)

            def sb_step(u):
                i, j, jj, n0, q0 = geom(u)
                first = (j == 4 * i + 3)
                last = (j == 0)
                s_, e_, x_, a_ = spb[u % 3], e32[u % 3], exb[u % 3], abf[u % 3]
                R0, R1, O0_, O1_ = psb[4], psb[5], psb[6], psb[7]
                if first:
                    P.op("tensor", [
                        lambda e: e.matmul(R0[:, :], zer[:, :], QT[0][:, q0:q0 + WT], start=True, stop=True),
                        lambda e: e.matmul(R1[:, :], zer[:, :], QT[1][:, q0:q0 + WT], start=True, stop=True),
                        lambda e: e.matmul(O0_[:, :], zer[:, :], QT[0][:, q0:q0 + WT], start=True, stop=True),
                        lambda e: e.matmul(O1_[:, :], zer[:, :], QT[1][:, q0:q0 + WT], start=True, stop=True)],
                        reads=[zer.b, QT[0].b, QT[1].b], writes=[R0.b, R1.b, O0_.b, O1_.b])
                g1 = []
                rd = [s_.b, cbf.b]
                if u > 0 and not first:
                    sp_, np_ = spb[(u - 1) % 3], geom(u - 1)[3]
                    g1 += [lambda e: e.matmul(R0[:, np_:], tcomp_bf, sp_[:, 0, np_:], start=False, stop=True),
                           lambda e: e.matmul(R1[:, np_:], tcomp_bf, sp_[:, 1, np_:], start=False, stop=True)]
                    rd.append(sp_.b)
                g1 += [lambda e: e.matmul(R0[:, n0:], tinc_bf, s_[:, 0, n0:], start=False, stop=True),
                       lambda e: e.matmul(R1[:, n0:], tinc_bf, s_[:, 1, n0:], start=False, stop=True)]
                P.op("tensor", g1, reads=rd, writes=[R0.b, R1.b])
                if u + 1 < len(units):
                    emit_exp(u + 1)
                P.op("scalar", lambda e: e.activation(x_[:, :, n0:], ps_all[:, 4:6, n0:], AF.Exp), reads=[R0.b, R1.b], writes=[x_.b])
                if u + 1 < len(units):
                    emit_ln(u + 1)
                P.op("vector", lambda e: e.tensor_tensor(a_[:, :, n0:], e_[:, :, n0:], x_[:, :, n0:], ALU.mult),
                     reads=[e_.b, x_.b], writes=[a_.b])
                P.op("tensor", [
                    lambda e: e.matmul(O0_[:, n0:], Vh[0][:, j * 128:(j + 1) * 128], a_[:, 0, n0:], start=False, stop=True),
                    lambda e: e.matmul(O1_[:, n0:], Vh[1][:, j * 128:(j + 1) * 128], a_[:, 1, n0:], start=False, stop=True)],
                    reads=[a_.b, Vh[0].b, Vh[1].b], writes=[O0_.b, O1_.b])
                if u + 2 < len(units):
                    emit_Z(u + 2)
                if last:
                    P.op("scalar", lambda e: e.activation(obf[:, :, :], ps_all[:, 6:8, :], AF.Copy), reads=[O0_.b, O1_.b], writes=[obf.b])
                    P.dma("sync", [lambda e, h=h: e.dma_start(
                        out=att_send_bf[i // 2][256 + h * 128:256 + (h + 1) * 128, (i % 2) * WT:(i % 2 + 1) * WT], in_=obf[:, h, :])
                        for h in range(2)], obf.b, reads=[obf.b], writes=[d_send[i // 2]])
                    if i % 2 == 1:
                        emit_ag(i // 2)

            for u in range(len(units)):
                sb_step(u)
            P.barrier()

        if debug and run_p2:
            tmpa0 = sb(root, "tmpa0", [128, 512], F32)
            for k in range(4):
                P.dma("sync", lambda e, k=k: e.dma_start(out=tmpa0[:, :], in_=att_all[0][k * 128:(k + 1) * 128, :]), tmpa0.b,
                      reads=[d_all[0]], writes=[tmpa0.b])
                P.dma("sync", lambda e, k=k: e.dma_start(out=dbg["att"][k * 128:(k + 1) * 128, :], in_=tmpa0[:, :]), tmpa0.b,
                      reads=[tmpa0.b], writes=[d_dbg])
        run_p3 = (stop_after is None) or stop_after.startswith("p3")
        if run_p3:
          with ExitStack() as ph:
            xs1 = [sb(ph, "p3xs", [128, D], F32)]
            xT3 = sb(ph, "p3xT", [128, KC, WT], F32)
            xT3.bs = [Buf(f"xT3{k}") for k in range(KC)]
            sq3 = sb(ph, "p3sq", [128, KC, WT], BF16)
            hT3 = sb(ph, "p3hT", [128, KC, WT], BF16)
            rstd3 = sb(ph, "p3rstd", [128, WT], F32)
            wsm = [sb(ph, f"wsm{i}", [128, KC * 128], BF16) for i in range(8)]
            cbuf = sb(ph, "cbuf", [128, 2 * NFC, 2], F32)
            Kmem = sb(ph, "Kmem", [128, 4, NMEM], BF16)
            Vmem = sb(ph, "Vmem", [128, 2, 512], BF16)
            P.op("vector", lambda e: e.memset(cbuf[:, :, :], 0.0), writes=[cbuf.b])

            def stream(items, slots, pf):
                loaded = []

                def ensure(k):
                    while len(loaded) <= min(k, len(items) - 1):
                        ap_, n_ = items[len(loaded)]
                        loaded.append(load_w(slots, ap_, [n_]))
                for k in range(len(items)):
                    ensure(k + pf)
                    yield loaded[k]

            def mm_group(pb, W, w, rhs_fn, nk, wbufs, rbufs):
                fns = [lambda e, kc=kc: e.matmul(pb[:, 0:W], w[:, kc * 128:(kc + 1) * 128], rhs_fn(kc),
                                                 start=(kc == 0), stop=(kc == nk - 1)) for kc in range(nk)]
                P.op("tensor", fns, reads=[w.b] + list(rbufs), writes=[pb.b])

            def resid_add(m, W, pb):
                P.op("vector", lambda e: e.tensor_tensor(xT3[:, m, 0:W], pb[:, 0:W], xT3[:, m, 0:W], ALU.add),
                     reads=[pb.b], writes=[xT3.bs[m]])

            def norm3(W, gcol0, outT):
                xT3.b.w = [t for b_ in xT3.bs for t in b_.w]
                rmsnorm_T(xT3, W, gcol0, outT, sq3, rstd3)
                for b_ in xT3.bs:
                    b_.r = b_.r + xT3.b.r
                xT3.b.r = []

            with ExitStack() as pm:
                mT = sb(pm, "mT", [128, KC, NMEM], F32)
                msq = sb(pm, "msq", [128, KC, NMEM], BF16)
                mnT = sb(pm, "mnT", [128, KC, NMEM], BF16)
                mr = sb(pm, "mr", [128, NMEM], F32)
                wxv_sb = sb(pm, "wxv_sb", [128, KC * 512], BF16)
                P.dma("gpsimd", lambda e: e.dma_start(out=wxv_sb[:, :], in_=wxv[:, :, :].rearrange("p k j -> p (k j)")),
                      wxv_sb.b, writes=[wxv_sb.b])
                load_T(xs1, lambda s_, n: memb[s_ * 128:s_ * 128 + n, :], 2, [128, 128], mT, "mem")
                rmsnorm_T(mT, NMEM, 2 * KC, mnT, msq, mr)
                for h, w in enumerate(stream([(wxk[h].rearrange("p k j -> p (k j)"), KC * 128) for h in range(4)], wsm, 3)):
                    pb = pbank()
                    mm_group(pb, NMEM, w, lambda kc: mnT[:, kc, :], KC, None, [mnT.b])
                    P.op("scalar", lambda e, pb=pb, h=h: e.activation(Kmem[:, h, :], pb[:, 0:NMEM], AF.Copy),
                         reads=[pb.b], writes=[Kmem.b])
                for blk in range(2):
                    pb = pbank()
                    fns = [lambda e, kc=kc, blk=blk, pb=pb: e.matmul(
                        pb[:, :], mnT[:, kc, blk * 128:(blk + 1) * 128], wxv_sb[:, kc * 512:(kc + 1) * 512],
                        start=(kc == 0), stop=(kc == KC - 1)) for kc in range(KC)]
                    P.op("tensor", fns, reads=[mnT.b, wxv_sb.b], writes=[pb.b])
                    P.op("scalar", lambda e, pb=pb, blk=blk: e.activation(Vmem[:, blk, :], pb[:, :], AF.Copy),
                         reads=[pb.b], writes=[Vmem.b])
                P.barrier()

            def dump_x(stage, src=None):
                if not debug:
                    return
                src = src or xT3
                rb = getattr(src, "bs", None) or [src.b]
                P.dma("gpsimd", lambda e: e.dma_start(out=dbg["x"][stage], in_=src[:, :, :].rearrange("p k w -> p (k w)")),
                      rb[0], reads=rb, writes=[d_dbg])

            def a_oa(h):
                return 4 * (h // 2) + (h % 2)

            def a_ob(h):
                return 4 * (h // 2) + 2 + (h % 2)

            def tile(W, row0, halo, t):
                nsub = (W + 127) // 128
                sizes = [min(128, W - 128 * k) for k in range(nsub)]
                xT3.b.w = []
                xT3.b.r = [t_ for b_ in xT3.bs for t_ in (b_.w + b_.r)]
                load_T(xs1, lambda s_, n: xo[row0 + s_ * 128:row0 + s_ * 128 + n, :], nsub, sizes, xT3, "p3")
                for b_ in xT3.bs:
                    b_.w = list(xT3.b.w)
                    b_.r = []
                if not halo and t == 0:
                    dump_x(0)
                norm3(W, 0, hT3)
                with ExitStack() as sc:
                    cand = [sb(sc, f"cand{q}", [128, 4, WT], BF16) for q in range(4)]
                    asel = sb(sc, "asel", [128, KC, WT], BF16)
                    asel.bs = [Buf(f"asel{k}") for k in range(4)]
                    merged = sb(sc, "merged", [128, KC, WT], BF16)
                    merged.bs = [Buf(f"mg{k}") for k in range(KC)]
                    sg = [sb(sc, f"sg{i}", [128, WT], F32) for i in range(4)]
                    tmpm = [sb(sc, f"tmpm{i}", [128, WT], F32) for i in range(2)]
                    qs = [1, 2, 3] if halo else [0, 1, 2, 3]
                    if short:
                        qs = [] if (halo or "nocand" in stop_after) else [0]
                        if halo or "nocand" in stop_after:
                            for qtr in range(4):
                                P.op("vector", lambda e, qtr=qtr: e.memset(asel[:, qtr * 4:(qtr + 1) * 4, 0:W], 0.0), writes=[asel.bs[qtr]])
                    for qtr in range(4):
                        for q in qs:
                            chunk, coff = (2 * q - 1, 1024 - HALO) if halo else (2 * q + t // 2, (t % 2) * WT)
                            src = att_all_bf[chunk][qtr * 512:(qtr + 1) * 512, coff:coff + W].rearrange("(a p) w -> p a w", p=128)
                            P.dma("sync", lambda e, q=q, src=src: e.dma_start(out=cand[q][:, :, 0:W], in_=src),
                                  cand[q].b, reads=[d_all[chunk]], writes=[cand[q].b])
                        dst = asel[:, qtr * 4:(qtr + 1) * 4, 0:W]
                        for n_, q in enumerate(qs):
                            if n_ == 0:
                                P.op("vector", lambda e, q=q, dst=dst: e.tensor_scalar(
                                    dst, cand[q][:, :, 0:W], mk[:, q:q + 1], None, ALU.mult),
                                    reads=[cand[q].b, mk.b], writes=[asel.bs[qtr]])
                            else:
                                P.op("vector", lambda e, q=q, dst=dst: e.scalar_tensor_tensor(
                                    dst, cand[q][:, :, 0:W], mk[:, q:q + 1], dst, ALU.mult, ALU.add),
                                    reads=[cand[q].b, mk.b], writes=[asel.bs[qtr]])
                    items = []
                    for m in range(KC):
                        items += [(wg[m].rearrange("p k j -> p (k j)"), KC * 128), (wg[KC + m].rearrange("p k j -> p (k j)"), KC * 128),
                                  (wpa[m].rearrange("p k j -> p (k j)"), 8 * 128), (wpb[m].rearrange("p k j -> p (k j)"), 8 * 128)]
                    ws = stream(items, wsm, 4)
                    for m in range(KC):
                        wga, wgb, wa, wb_ = next(ws), next(ws), next(ws), next(ws)
                        A, B, C, Dd = pbank(), pbank(), pbank(), pbank()
                        mm_group(A, W, wga, lambda kc: hT3[:, kc, 0:W], KC, None, [hT3.b])
                        mm_group(B, W, wgb, lambda kc: hT3[:, kc, 0:W], KC, None, [hT3.b])
                        mm_group(C, W, wa, lambda kc: asel[:, a_oa(kc), 0:W], 8, None, asel.bs)
                        mm_group(Dd, W, wb_, lambda kc: asel[:, a_ob(kc), 0:W], 8, None, asel.bs)
                        sga, sgb = sg[(2 * m) % 4], sg[(2 * m + 1) % 4]
                        P.op("scalar", lambda e, sga=sga, A=A: e.activation(sga[:, 0:W], A[:, 0:W], AF.Sigmoid), reads=[A.b], writes=[sga.b])
                        P.op("scalar", lambda e, sgb=sgb, B=B: e.activation(sgb[:, 0:W], B[:, 0:W], AF.Sigmoid), reads=[B.b], writes=[sgb.b])
                        t0_, t1_ = tmpm
                        P.op("vector", lambda e, sga=sga, C=C: e.tensor_tensor(t0_[:, 0:W], C[:, 0:W], sga[:, 0:W], ALU.mult),
                             reads=[sga.b, C.b], writes=[t0_.b])
                        P.op("vector", lambda e, sgb=sgb, Dd=Dd: e.tensor_tensor(t1_[:, 0:W], Dd[:, 0:W], sgb[:, 0:W], ALU.mult),
                             reads=[sgb.b, Dd.b], writes=[t1_.b])
                        P.op("vector", lambda e, m=m: e.tensor_tensor(merged[:, m, 0:W], t0_[:, 0:W], t1_[:, 0:W], ALU.add),
                             reads=[t0_.b, t1_.b], writes=[merged.bs[m]])
                    for m, w in enumerate(stream([(wout[m].rearrange("p k j -> p (k j)"), KC * 128) for m in range(KC)], wsm, 4)):
                        pb = pbank()
                        mm_group(pb, W, w, lambda kc: merged[:, kc, 0:W], KC, None, merged.bs)
                        resid_add(m, W, pb)
                    if not halo and t == 0:
                        dump_x(1)
                        if stop_after in ("p3mixshort", "p3mixnocandshort"):
                            dump_x(2, hT3)
                            P.dma("gpsimd", lambda e: e.dma_start(out=dbg["x"][0][:, 0:4 * WT], in_=cand[0][:, :, :].rearrange("p k w -> p (k w)")),
                                  cand[0].b, reads=[cand[0].b], writes=[d_dbg])
                            P.dma("gpsimd", lambda e: e.dma_start(out=dbg["x"][0][:, 4 * WT:4 * WT + 8], in_=mk[:, :]),
                                  mk.b, reads=[mk.b], writes=[d_dbg])
                            dump_x(3, asel)
                            dump_x(4, merged)
                    P.barrier()
                if stop_after in ("p3mixshort", "p3mixnocandshort"):
                    return
                norm3(W, KC, hT3)
                with ExitStack() as sc:
                    qx = sb(sc, "qx", [128, 4, WT], BF16)
                    Ex = [sb(sc, f"Ex{i}", [128, 2, WT], BF16) for i in range(2)]
                    ox = sb(sc, "ox", [128, 4, WT], BF16)
                    ox.bs = [Buf(f"ox{k}") for k in range(4)]
                    rx = [sb(sc, f"rx{i}", [128, WT], F32) for i in range(2)]
                    for h, w in enumerate(stream([(wxq[h].rearrange("p k j -> p (k j)"), KC * 128) for h in range(4)], wsm, 4)):
                        pb = pbank()
                        mm_group(pb, W, w, lambda kc: hT3[:, kc, 0:W], KC, None, [hT3.b])
                        P.op("scalar", lambda e, pb=pb, h=h: e.activation(qx[:, h, 0:W], pb[:, 0:W], AF.Copy), reads=[pb.b], writes=[qx.b])
                    for h in range(4):
                        S0, S1, Dn, O = pbank(), pbank(), pbank(), pbank()
                        E = Ex[h % 2]
                        P.op("tensor", [lambda e, h=h, S0=S0: e.matmul(S0[:, 0:W], Kmem[:, h, 0:128], qx[:, h, 0:W], start=True, stop=True),
                                        lambda e, h=h, S1=S1: e.matmul(S1[:, 0:W], Kmem[:, h, 128:256], qx[:, h, 0:W], start=True, stop=True)],
                             reads=[Kmem.b, qx.b], writes=[S0.b, S1.b])
                        P.op("scalar", lambda e, E=E, S0=S0: e.activation(E[:, 0, 0:W], S0[:, 0:W], AF.Exp, scale=128.0 ** -0.5),
                             reads=[S0.b], writes=[E.b])
                        P.op("scalar", lambda e, E=E, S1=S1: e.activation(E[:, 1, 0:W], S1[:, 0:W], AF.Exp, scale=128.0 ** -0.5),
                             reads=[S1.b], writes=[E.b])
                        P.op("tensor", [lambda e, E=E, Dn=Dn: e.matmul(Dn[:, 0:W], ones_bf, E[:, 0, 0:W], start=True, stop=False),
                                        lambda e, E=E, Dn=Dn: e.matmul(Dn[:, 0:W], ones_bf, E[:, 1, 0:W], start=False, stop=True),
                                        lambda e, E=E, O=O, h=h: e.matmul(O[:, 0:W], Vmem[:, 0, h * 128:(h + 1) * 128], E[:, 0, 0:W], start=True, stop=False),
                                        lambda e, E=E, O=O, h=h: e.matmul(O[:, 0:W], Vmem[:, 1, h * 128:(h + 1) * 128], E[:, 1, 0:W], start=False, stop=True)],
                             reads=[E.b, Vmem.b, cbf.b], writes=[Dn.b, O.b])
                        r_ = rx[h % 2]
                        P.op("vector", lambda e, r_=r_, Dn=Dn: e.reciprocal(r_[:, 0:W], Dn[:, 0:W]), reads=[Dn.b], writes=[r_.b])
                        P.op("vector", lambda e, r_=r_, O=O, h=h: e.tensor_tensor(ox[:, h, 0:W], O[:, 0:W], r_[:, 0:W], ALU.mult),
                             reads=[O.b, r_.b], writes=[ox.bs[h]])
                    for m, w in enumerate(stream([(wxo[m].rearrange("p k j -> p (k j)"), 4 * 128) for m in range(KC)], wsm, 4)):
                        pb = pbank()
                        mm_group(pb, W, w, lambda kc: ox[:, kc, 0:W], 4, None, ox.bs)
                        resid_add(m, W, pb)
                    if not halo and t == 0:
                        dump_x(2)
                    P.barrier()
                norm3(W, 3 * KC, hT3)
                with ExitStack() as sc:
                    act = sb(sc, "act", [128, NFC, WT], BF16)
                    act.bs = [Buf(f"act{k}") for k in range(NFC)]
                    ub = [sb(sc, f"ub{i}", [128, WT + 2], F32) for i in range(3)]
                    yb = [sb(sc, f"yb{i}", [128, WT], F32) for i in range(4)]
                    sgf = [sb(sc, f"sgf{i}", [128, WT], F32) for i in range(2)]
                    wbg = [sb(sc, f"wbg{i}", [128, NFC * 128], BF16) for i in range(2)]
                    items = []
                    for c in range(NFC):
                        items += [(wup[c].rearrange("p k j -> p (k j)"), KC * 128), (wup[NFC + c].rearrange("p k j -> p (k j)"), KC * 128)]
                    ws = stream(items, wsm, 5)
                    cnt = 0
                    for c in range(NFC):
                        ys = []
                        for half in range(2):
                            uc = half * NFC + c
                            w = next(ws)
                            pb = pbank()
                            mm_group(pb, W, w, lambda kc: hT3[:, kc, 0:W], KC, None, [hT3.b])
                            cnt += 1
                            u_, y_ = ub[cnt % 3], yb[cnt % 4]
                            ys.append(y_)
                            P.op("vector", lambda e, u_=u_, uc=uc: e.tensor_copy(u_[:, 0:2], cbuf[:, uc, :]), reads=[cbuf.b], writes=[u_.b])
                            P.op("scalar", lambda e, u_=u_, pb=pb: e.activation(u_[:, 2:W + 2], pb[:, 0:W], AF.Copy), reads=[pb.b], writes=[u_.b])
                            P.op("vector", lambda e, u_=u_, uc=uc: e.tensor_copy(cbuf[:, uc, :], u_[:, W:W + 2]), reads=[u_.b], writes=[cbuf.b])
                            if not halo:
                                P.op("scalar", lambda e, y_=y_, pb=pb, uc=uc: e.activation(
                                    y_[:, 0:W], pb[:, 0:W], AF.Identity, bias=cb[:, uc:uc + 1], scale=cw[:, uc * 3 + 2:uc * 3 + 3]),
                                    reads=[pb.b, cb.b, cw.b], writes=[y_.b])
                                P.op("vector", lambda e, y_=y_, u_=u_, uc=uc: e.scalar_tensor_tensor(
                                    y_[:, 0:W], u_[:, 1:W + 1], cw[:, uc * 3 + 1:uc * 3 + 2], y_[:, 0:W], ALU.mult, ALU.add),
                                    reads=[u_.b, cw.b], writes=[y_.b])
                                P.op("vector", lambda e, y_=y_, u_=u_, uc=uc: e.scalar_tensor_tensor(
                                    y_[:, 0:W], u_[:, 0:W], cw[:, uc * 3:uc * 3 + 1], y_[:, 0:W], ALU.mult, ALU.add),
                                    reads=[u_.b, cw.b], writes=[y_.b])
                        if not halo:
                            s_ = sgf[c % 2]
                            P.op("scalar", lambda e, s_=s_, yg=ys[0]: e.activation(s_[:, 0:W], yg[:, 0:W], AF.Silu), reads=[ys[0].b], writes=[s_.b])
                            P.op("vector", lambda e, s_=s_, yv=ys[1], c=c: e.tensor_tensor(act[:, c, 0:W], s_[:, 0:W], yv[:, 0:W], ALU.mult),
                                 reads=[s_.b, ys[1].b], writes=[act.bs[c]])
                    if halo:
                        P.op("vector", lambda e: e.tensor_scalar(cbuf[:, :, :], cbuf[:, :, :], mk[:, 4:5], None, ALU.mult),
                             reads=[mk.b], writes=[cbuf.b])
                    else:
                        for m, w in enumerate(stream([(wdn[m].rearrange("p k j -> p (k j)"), NFC * 128) for m in range(KC)], wbg, 1)):
                            pb = pbank()
                            mm_group(pb, W, w, lambda kc: act[:, kc, 0:W], NFC, None, act.bs)
                            resid_add(m, W, pb)
                    if not halo and t == 0:
                        dump_x(3)
                    P.barrier()
                if halo:
                    return
                with ExitStack() as sc:
                    yT = sb(sc, "yT", [128, KC, WT], F32)
                    norm3(W, 4 * KC, yT)
                    if t == 0:
                        dump_x(4, yT)
                    xs = xs1[0]
                    for s_ in range(4):
                        for g in range(4):
                            pb = pbank()
                            fns = [lambda e, pb=pb, j=j, g=g, s_=s_: e.transpose(
                                pb[:, j * 128:(j + 1) * 128], yT[:, 4 * g + j, s_ * 128:(s_ + 1) * 128], ident32) for j in range(4)]
                            P.op("tensor", fns, reads=[yT.b, c32.b], writes=[pb.b])
                            if g % 2 == 0:
                                P.op("vector", lambda e, pb=pb, g=g: e.tensor_copy(xs[:, g * 512:(g + 1) * 512], pb[:, :]),
                                     reads=[pb.b], writes=[xs.b])
                            else:
                                P.op("scalar", lambda e, pb=pb, g=g: e.activation(xs[:, g * 512:(g + 1) * 512], pb[:, :], AF.Copy),
                                     reads=[pb.b], writes=[xs.b])
                        r0 = t * WT + s_ * 128
                        P.dma("sync", lambda e, r0=r0: e.dma_start(out=y[r0:r0 + 128, :], in_=xs[:, :]), xs.b, reads=[xs.b], writes=[d_y])
                    P.barrier()

            tile(HALO, 0, True, 0)
            ntile3 = 1 if (stop_after and ("short" in stop_after or "one" in stop_after)) else NTILE
            for t in range(ntile3):
                tile(WT, HALO + t * WT, False, t)
            P.barrier()

        if debug and stop_after is not None and stop_after != "c0":
            tmpd = sb(root, "tmpd", [128, S], BF16)
            for c in range(8):
                P.dma("sync", lambda e, c=c: e.dma_start(out=tmpd[:, :], in_=qkT[c]), tmpd.b, reads=[d_qk], writes=[tmpd.b])
                P.dma("sync", lambda e, c=c: e.dma_start(out=dbg["qk"][c], in_=tmpd[:, :]), tmpd.b, reads=[tmpd.b], writes=[d_dbg])
            for h in range(4):
                P.dma("sync", lambda e, h=h: e.dma_start(out=tmpd[:, :], in_=vS[h].rearrange("p b d -> p (b d)")),
                      tmpd.b, reads=[d_v], writes=[tmpd.b])
                P.dma("sync", lambda e, h=h: e.dma_start(out=dbg["v"][h].rearrange("p b d -> p (b d)"), in_=tmpd[:, :]),
                      tmpd.b, reads=[tmpd.b], writes=[d_dbg])

        if debug and run_p2:
            tmpa = sb(root, "tmpa", [128, 512], F32)
            for k in range(4):
                P.dma("sync", lambda e, k=k: e.dma_start(out=tmpa[:, :], in_=att_all[0][k * 128:(k + 1) * 128, :]), tmpa.b,
                      reads=[d_all[0]], writes=[tmpa.b])
                P.dma("sync", lambda e, k=k: e.dma_start(out=dbg["x"][4][:, k * 512:(k + 1) * 512], in_=tmpa[:, :]), tmpa.b,
                      reads=[tmpa.b], writes=[d_dbg])
        with nc.Block() as block:
            P.finish(block)
        print("instructions:", P.ninst, "sems:", P.nsem)
    return nc


def _blk(W, cols=None):
    if cols is not None:
        W = W[:, cols]
    K, N = W.shape
    return np.ascontiguousarray(W.reshape(K // 128, 128, N // 128, 128).transpose(2, 1, 0, 3))


def _rhs(W, cols=None):
    if cols is not None:
        W = W[:, cols]
    K, N = W.shape
    return np.ascontiguousarray(W.reshape(K // 128, 128, N).transpose(1, 0, 2))


def _cols(g):
    return np.ascontiguousarray(np.asarray(g, np.float32).reshape(-1, 128).T)


def _rope_tables():
    inv = (np.float32(500000.0) ** (-np.arange(0, 16, 2, dtype=np.float32) / np.float32(16))).astype(np.float32)
    ang = (np.arange(S, dtype=np.float32)[:, None] * inv[None, :]).astype(np.float32)
    cos = np.cos(ang).astype(np.float32).T
    sin = np.sin(ang).astype(np.float32).T
    C = np.ones((128, S), np.float32)
    Sg = np.zeros((128, S), np.float32)
    for base in (0, 64):
        C[base:base + 8] = cos
        C[base + 8:base + 16] = cos
        Sg[base:base + 8] = -sin
        Sg[base + 8:base + 16] = sin
    return C, Sg


def _consts():
    ident = np.eye(128, dtype=np.float32)
    perm = np.zeros((128, 128), np.float32)
    for base in (0, 64):
        for i in range(8):
            perm[base + 8 + i, base + i] = 1.0
            perm[base + i, base + 8 + i] = 1.0
    j = np.arange(128)[:, None]
    s = np.arange(128)[None, :]
    tinc = np.where(j >= s, -1.0, 0.0).astype(np.float32)
    tcomp = np.where(j < s, -1.0, 0.0).astype(np.float32)
    ones = np.ones((128, 128), np.float32)
    return np.concatenate([ident, perm, tinc, tcomp, ones], axis=1)


def prep_inputs(inp):
    f = lambda k: np.asarray(inp[k], np.float32)
    x, mem = f("x"), f("mem")
    w_in = f("w_in")[0]
    ropeC, ropeS = _rope_tables()
    consts = _consts()
    gains = np.concatenate([_cols(f("norm_mix_g")[0]), _cols(f("norm_x_g")[0]), _cols(f("norm_mem_g")[0]),
                            _cols(f("norm_ffn_g")[0]), _cols(f("final_norm_g")), _cols(f("da_subln_g")[0])], axis=1)
    cwt = f("conv_w")[0]
    convw = np.ascontiguousarray(cwt.T.reshape(2 * NFC, 128, 3).transpose(1, 0, 2)).reshape(128, 2 * NFC * 3)
    convb = _cols(f("conv_b")[0])
    lam4 = np.concatenate([f("lambda_q1")[0], f("lambda_k1")[0], f("lambda_q2")[0], f("lambda_k2")[0]])
    lam4 = np.ascontiguousarray(np.broadcast_to(lam4[None, :], (128, 256)))
    shared = dict(
        wg=_blk(w_in[:, 6144:10240]),
        wpa=_blk(f("w_proj_a")[0]), wpb=_blk(f("w_proj_b")[0]), wout=_blk(f("w_out")[0]),
        wxq=_blk(f("w_xq")[0]), wxk=_blk(f("w_xkv")[0][:, :512]), wxv=_rhs(f("w_xkv")[0][:, 512:]),
        wxo=_blk(f("w_xo")[0]), wup=_blk(f("w_up")[0]), wdn=_blk(f("w_down")[0]),
        gains=np.ascontiguousarray(gains), convw=convw, convb=convb, lam4=lam4,
        ropeC=ropeC, ropeS=ropeS, consts=consts,
    )
    maps = []
    for c in range(8):
        b, r = c // 4, c % 4
        h0, h1 = 2 * r, 2 * r + 1
        def hc(base, h):
            return np.arange(base + h * 128, base + (h + 1) * 128)
        qk_cols = np.concatenate([hc(0, h0), hc(1024, h0), hc(0, h1), hc(1024, h1),
                                  hc(3072, h0), hc(4096, h0), hc(3072, h1), hc(4096, h1)])
        v_cols = np.concatenate([hc(2048, h0), hc(2048, h1), hc(5120, h0), hc(5120, h1)])
        xo = np.zeros((HALO + NTILE * WT, D), np.float32)
        lo = 2048 * r - HALO
        if r == 0:
            xo[HALO:] = x[b, 0:2048]
        else:
            xo[:] = x[b, lo:lo + HALO + 2048]
        masks = np.zeros((128, 8), np.float32)
        masks[:, r] = 1.0
        masks[:, 4] = 0.0 if r == 0 else 1.0
        m = dict(shared)
        m.update(xb=np.ascontiguousarray(x[b]), xo=xo, memb=np.ascontiguousarray(mem[b]),
                 wqk=_blk(w_in, qk_cols), wv=_rhs(w_in, v_cols), masks=masks)
        maps.append(m)
    return maps


_NC_CACHE = {}


def kernel(**inputs):
    maps = prep_inputs(inputs)
    if "nc" not in _NC_CACHE:
        _NC_CACHE["nc"] = build()
    nc = _NC_CACHE["nc"]
    res = run_bass_kernel_spmd(nc, maps, core_ids=list(range(8)))
    out = np.zeros((2, S, D), np.float32)
    for c in range(8):
        b, r = c // 4, c % 4
        out[b, 2048 * r:2048 * (r + 1)] = np.asarray(res.results[c]["y"], np.float32)
    return out
```

```python
import math
from contextlib import ExitStack

import numpy as np
import concourse.bass as bass
import concourse.mybir as mybir
from concourse.bass_utils import run_bass_kernel_spmd

F32 = mybir.dt.float32
BF16 = mybir.dt.bfloat16
AF = mybir.ActivationFunctionType
ALU = mybir.AluOpType

D = 2048
S = 8192
KC = 16
NMEM = 256
DFF = 5632
NFC = 44
HALO = 16
WT = 512
NTILE = 4
EPS = 1e-6
LAM_INIT = 0.8 - 0.6 * math.exp(0.0)
SEM_ROT = 12000
NATT = HALO + S


class Buf:
    __slots__ = ("name", "w", "r", "multi", "excl")

    def __init__(self, name, multi=False, excl=False):
        self.name = name
        self.w = []
        self.r = []
        self.multi = multi
        self.excl = excl


class Eng:
    def __init__(self, prog, name):
        self.prog = prog
        self.name = name
        self.q = []
        self.sem = None
        self.cnt = 0
        self.waited = {}

    def newsem(self):
        self.sem = self.prog.new_sem(self.name)
        self.cnt = 0


class Prog:
    def __init__(self, nc, stack):
        self.nc = nc
        self.stack = stack
        self.nsem = 0
        self.dsem = {}
        self.sem_owner = {}
        self.engs = {n: Eng(self, n) for n in ("tensor", "scalar", "vector", "gpsimd", "sync")}
        for e in self.engs.values():
            e.newsem()
        self.ninst = 0

    def new_sem(self, name):
        self.nsem += 1
        sem = self.stack.enter_context(self.nc.semaphore(f"s{self.nsem}{name[:2]}"))
        self.sem_owner[id(sem)] = name
        return sem

    def _waits(self, eng, toks):
        need = {}
        for (sem, val) in toks:
            k = id(sem)
            if eng.waited.get(k, 0) >= val:
                continue
            if k not in need or need[k][1] < val:
                need[k] = (sem, val)
        for k, (sem, val) in need.items():
            eng.waited[k] = val
            eng.q.append(lambda e, sem=sem, val=val: e.wait_ge(sem, val))
            self.ninst += 1

    def _deps(self, reads, writes, extra, engname=None):
        toks = list(extra)
        for b in reads:
            toks += b.w
            if b.excl:
                toks += [t for t in b.r if self.sem_owner.get(id(t[0])) != engname]
        for b in writes:
            if not b.multi:
                toks += b.w
                toks += b.r
        return toks

    def _commit(self, tok, reads, writes):
        for b in writes:
            if b.multi:
                b.w = [t for t in b.w if t[0] is not tok[0]] + [tok]
            else:
                b.w = [tok]
                b.r = []
        for b in reads:
            if b not in writes:
                b.r = [t for t in b.r if t[0] is not tok[0]] + [tok]

    def op(self, engname, fns, reads=(), writes=(), extra=()):
        if not isinstance(fns, (list, tuple)):
            fns = [fns]
        eng = self.engs[engname]
        self._waits(eng, self._deps(reads, writes, extra, engname))
        for fn in fns[:-1]:
            eng.q.append(lambda e, fn=fn: fn(e))
        if eng.cnt >= SEM_ROT:
            eng.newsem()
        eng.cnt += 1
        sem = eng.sem
        eng.q.append(lambda e, fn=fns[-1], sem=sem: fn(e).then_inc(sem, 1))
        self.ninst += len(fns)
        tok = (sem, eng.cnt)
        self._commit(tok, reads, writes)
        return tok

    def dma(self, engname, fns, owner, reads=(), writes=(), extra=()):
        if not isinstance(fns, (list, tuple)):
            fns = [fns]
        eng = self.engs[engname]
        self._waits(eng, self._deps(reads, writes, extra, engname))
        st = self.dsem.setdefault(id(owner), [None, 0, owner])
        if st[0] is None or st[1] + 16 * len(fns) > 16 * 1200:
            st[0] = self.new_sem("d")
            st[1] = 0
        sem = st[0]
        for fn in fns:
            eng.q.append(lambda e, fn=fn, sem=sem: fn(e).then_inc(sem, 16))
        self.ninst += len(fns)
        st[1] += 16 * len(fns)
        tok = (sem, st[1])
        self._commit(tok, reads, writes)
        return tok

    def all_tokens(self):
        toks = [(e.sem, e.cnt) for e in self.engs.values() if e.cnt > 0]
        toks += [(st[0], st[1]) for st in self.dsem.values() if st[0] is not None and st[1] > 0]
        return toks

    def barrier(self):
        toks = self.all_tokens()
        for e in self.engs.values():
            self._waits(e, toks)

    def finish(self, block):
        self.barrier()
        engs = self.engs

        @block.tensor
        def _(e):
            for fn in engs["tensor"].q:
                fn(e)

        @block.scalar
        def _(e):
            for fn in engs["scalar"].q:
                fn(e)

        @block.vector
        def _(e):
            for fn in engs["vector"].q:
                fn(e)

        @block.gpsimd
        def _(e):
            for fn in engs["gpsimd"].q:
                fn(e)

        @block.sync
        def _(e):
            for fn in engs["sync"].q:
                fn(e)


class T:
    def __init__(self, ap_tensor, name, excl=False):
        self.t = ap_tensor
        self.b = Buf(name, excl=excl)

    def __getitem__(self, idx):
        return self.t[idx]


def build(stop_after=None, debug=False):
    nc = bass.Bass("TRN2", target_bir_lowering=False)

    def din(name, shape):
        return nc.dram_tensor(name, list(shape), F32, kind="ExternalInput")

    xb = din("xb", [S, D])
    xo = din("xo", [HALO + NTILE * WT, D])
    memb = din("memb", [NMEM, D])
    wqk = din("wqk", [8, 128, KC, 128])
    wv = din("wv", [128, KC, 512])
    wg = din("wg", [32, 128, KC, 128])
    wpa = din("wpa", [16, 128, 8, 128])
    wpb = din("wpb", [16, 128, 8, 128])
    wout = din("wout", [16, 128, KC, 128])
    wxq = din("wxq", [4, 128, KC, 128])
    wxk = din("wxk", [4, 128, KC, 128])
    wxv = din("wxv", [128, KC, 512])
    wxo = din("wxo", [16, 128, 4, 128])
    wup = din("wup", [2 * NFC, 128, KC, 128])
    wdn = din("wdn", [16, 128, NFC, 128])
    gains = din("gains", [128, 5 * KC + 1])
    convw = din("convw", [128, 2 * NFC * 3])
    convb = din("convb", [128, 2 * NFC])
    lam4 = din("lam4", [128, 4 * 64])
    ropeC = din("ropeC", [128, S])
    ropeS = din("ropeS", [128, S])
    consts = din("consts", [128, 5 * 128])
    masks = din("masks", [128, 8])
    y = nc.dram_tensor("y", [NTILE * WT, D], F32, kind="ExternalOutput")

    qkT = nc.dram_tensor("qkT_s", [8, 128, S], BF16)
    vS = nc.dram_tensor("v_s", [4, 128, 64, 128], BF16)
    att_send = [nc.dram_tensor(f"att_send{k}", [512, 512], F32) for k in range(8)]
    att_all = [nc.dram_tensor(f"att_all{k}", [4 * 512, 512], F32) for k in range(8)]
    att_send_bf = [t[:, :].bitcast(BF16) for t in att_send]
    att_all_bf = [t[:, :].bitcast(BF16) for t in att_all]

    dbg = {}
    if debug:
        dbg["qk"] = nc.dram_tensor("dbg_qk", [8, 128, S], BF16, kind="ExternalOutput")
        dbg["v"] = nc.dram_tensor("dbg_v", [4, 128, 64, 128], BF16, kind="ExternalOutput")
        dbg["att"] = nc.dram_tensor("dbg_att", [512, 512], F32, kind="ExternalOutput")
        dbg["x"] = nc.dram_tensor("dbg_x", [5, 128, KC * WT], F32, kind="ExternalOutput")

    root = ExitStack()
    with root:
        P = Prog(nc, root)

        uid = [0]

        def sb(stack, name, shape, dt):
            uid[0] += 1
            return T(stack.enter_context(nc.sbuf_tensor(f"{name}_{uid[0]}", list(shape), dt)), name)

        ps_all = root.enter_context(nc.psum_tensor("ps_all", [128, 8, 512], F32))
        psb = [T(ps_all[:, i, :], f"ps{i}", excl=True) for i in range(8)]
        ps_rr = [0]

        def pbank():
            ps_rr[0] = (ps_rr[0] + 1) % 8
            return psb[ps_rr[0]]

        d_qk = Buf("d_qk", multi=True)
        d_v = Buf("d_v", multi=True)
        d_send = [Buf(f"d_send{k}", multi=True) for k in range(8)]
        d_all = [Buf(f"d_all{k}", multi=True) for k in range(8)]
        d_y = Buf("d_y", multi=True)
        d_dbg = Buf("d_dbg", multi=True)

        c32 = sb(root, "c32", [128, 5 * 128], F32)
        cbf = sb(root, "cbf", [128, 5 * 128], BF16)
        gn = sb(root, "gn", [128, 5 * KC + 1], F32)
        cw = sb(root, "cw", [128, 2 * NFC * 3], F32)
        cb = sb(root, "cb", [128, 2 * NFC], F32)
        mk = sb(root, "mk", [128, 8], F32)
        lm = sb(root, "lm", [128, 256], F32)
        lmp = sb(root, "lmp", [128, 128], F32)
        lms = sb(root, "lms", [128, 8], F32)
        P.dma("sync", lambda e: e.dma_start(out=c32[:, :], in_=consts[:, :]), c32.b, writes=[c32.b])
        P.dma("sync", lambda e: e.dma_start(out=gn[:, :], in_=gains[:, :]), gn.b, writes=[gn.b])
        P.dma("sync", lambda e: e.dma_start(out=cw[:, :], in_=convw[:, :]), cw.b, writes=[cw.b])
        P.dma("sync", lambda e: e.dma_start(out=cb[:, :], in_=convb[:, :]), cb.b, writes=[cb.b])
        P.dma("sync", lambda e: e.dma_start(out=mk[:, :], in_=masks[:, :]), mk.b, writes=[mk.b])
        P.dma("sync", lambda e: e.dma_start(out=lm[:, :], in_=lam4[:, :]), lm.b, writes=[lm.b])
        P.op("vector", lambda e: e.tensor_copy(cbf[:, :], c32[:, :]), reads=[c32.b], writes=[cbf.b])
        ident32 = c32[:, 0:128]
        perm_bf = cbf[:, 128:256]
        tinc_bf = cbf[:, 256:384]
        tcomp_bf = cbf[:, 384:512]
        ones_bf = cbf[:, 512:640]
        P.op("vector", lambda e: e.tensor_tensor(lmp[:, 0:64], lm[:, 0:64], lm[:, 64:128], ALU.mult),
             reads=[lm.b], writes=[lmp.b])
        P.op("vector", lambda e: e.tensor_tensor(lmp[:, 64:128], lm[:, 128:192], lm[:, 192:256], ALU.mult),
             reads=[lm.b], writes=[lmp.b])
        P.op("vector", lambda e: e.tensor_reduce(lms[:, 0:1], lmp[:, 0:64], mybir.AxisListType.X, ALU.add),
             reads=[lmp.b], writes=[lms.b])
        P.op("vector", lambda e: e.tensor_reduce(lms[:, 1:2], lmp[:, 64:128], mybir.AxisListType.X, ALU.add),
             reads=[lmp.b], writes=[lms.b])
        P.op("scalar", lambda e: e.activation(lms[:, 2:4], lms[:, 0:2], AF.Exp), reads=[lms.b], writes=[lms.b])
        P.op("vector", lambda e: e.scalar_tensor_tensor(lms[:, 5:6], lms[:, 3:4], -LAM_INIT, lms[:, 2:3],
                                                         ALU.add, ALU.subtract),
             reads=[lms.b], writes=[lms.b])
        P.op("vector", lambda e: e.tensor_scalar(lms[:, 6:7], gn[:, 5 * KC:5 * KC + 1], 1.0 - LAM_INIT, None,
                                                  ALU.mult),
             reads=[gn.b, lms.b], writes=[lms.b])
        neglam = lms[:, 5:6]
        agmark = sb(root, "agmark", [128, 8], F32)
        onec = sb(root, "onec", [128, 1], F32)
        P.op("vector", lambda e: e.memset(onec[:, :], 1.0), writes=[onec.b])
        epsc = sb(root, "epsc", [128, 1], F32)
        P.op("vector", lambda e: e.memset(epsc[:, :], EPS), writes=[epsc.b])
        gsub8 = lms[:, 6:7]

        def load_T(stack_tmp, src_rows_ap_fn, nsub, sizes, xT, tag=None):
            prev_acc = list(xT.b.w) + list(xT.b.r)
            xT.b.multi = True
            for s in range(nsub):
                n = sizes[s]
                xs = stack_tmp[s % len(stack_tmp)]
                P.dma("sync", lambda e, xs=xs, s=s, n=n: e.dma_start(out=xs[0:n, :], in_=src_rows_ap_fn(s, n)),
                      xs.b, writes=[xs.b])
                for g in range(4):
                    pb = pbank()
                    fns = []
                    for j in range(4):
                        kc = 4 * g + j
                        fns.append(lambda e, pb=pb, xs=xs, kc=kc, j=j, n=n: e.transpose(
                            pb[:, j * 128:j * 128 + n], xs[0:n, kc * 128:(kc + 1) * 128], ident32[0:n, 0:n]))
                    P.op("tensor", fns, reads=[xs.b, c32.b], writes=[pb.b])
                    off = sum(sizes[:s])
                    eng = "vector" if g % 2 == 0 else "scalar"
                    if eng == "vector":
                        P.op("vector", lambda e, pb=pb, g=g, off=off, n=n: e.tensor_copy(
                            xT[:, 4 * g:4 * g + 4, off:off + n],
                            pb[:, :].rearrange("p (j t) -> p j t", j=4)[:, :, 0:n]),
                            reads=[pb.b], writes=[xT.b], extra=prev_acc)
                    else:
                        P.op("scalar", lambda e, pb=pb, g=g, off=off, n=n: e.activation(
                            xT[:, 4 * g:4 * g + 4, off:off + n],
                            pb[:, :].rearrange("p (j t) -> p j t", j=4)[:, :, 0:n], AF.Copy),
                            reads=[pb.b], writes=[xT.b], extra=prev_acc)
            xT.b.multi = False

        def rmsnorm_T(xT, W, gcol0, outT, sq, rstd, nfeat=D):
            nk = nfeat // 128
            P.op("scalar", lambda e: e.activation(sq[:, 0:nk, 0:W], xT[:, 0:nk, 0:W], AF.Square),
                 reads=[xT.b], writes=[sq.b])
            pb = pbank()
            fns = [lambda e, kc=kc: e.matmul(pb[:, 0:W], ones_bf, sq[:, kc, 0:W], start=(kc == 0), stop=(kc == nk - 1))
                   for kc in range(nk)]
            P.op("tensor", fns, reads=[sq.b, cbf.b], writes=[pb.b])
            P.op("scalar", lambda e: e.activation(rstd[:, 0:W], pb[:, 0:W], AF.Sqrt, bias=epsc[:, 0:1], scale=1.0 / nfeat),
                 reads=[pb.b, epsc.b], writes=[rstd.b])
            P.op("vector", lambda e: e.reciprocal(rstd[:, 0:W], rstd[:, 0:W]),
                 reads=[rstd.b], writes=[rstd.b])
            for kc in range(nk):
                P.op("vector", lambda e, kc=kc: e.scalar_tensor_tensor(
                    outT[:, kc, 0:W], xT[:, kc, 0:W], gn[:, gcol0 + kc:gcol0 + kc + 1], rstd[:, 0:W],
                    ALU.mult, ALU.mult), reads=[xT.b, rstd.b, gn.b], writes=[outT.b])

        wq_rr = [0]

        def load_w(slots, dram_ap, shape_free):
            wq_rr[0] += 1
            slot = slots[wq_rr[0] % len(slots)]
            n = 1
            for s_ in shape_free:
                n *= s_
            P.dma("gpsimd", lambda e, slot=slot: e.dma_start(out=slot[:, 0:n], in_=dram_ap), slot.b, writes=[slot.b])
            return slot

        with ExitStack() as ph:
            if stop_after == "c0":
                raise_skip = True
            wqk_sb = [sb(ph, f"wqk{c}", [128, KC * 128], BF16) for c in range(8)]
            wv_sb = sb(ph, "wv_sb", [128, KC * 512], BF16)
            for c in range(8):
                P.dma("gpsimd", lambda e, c=c: e.dma_start(
                    out=wqk_sb[c][:, :], in_=wqk[c].rearrange("p k j -> p (k j)")), wqk_sb[c].b, writes=[wqk_sb[c].b])
            P.dma("gpsimd", lambda e: e.dma_start(out=wv_sb[:, :], in_=wv[:, :, :].rearrange("p k j -> p (k j)")),
                  wv_sb.b, writes=[wv_sb.b])
            xs_st = [sb(ph, f"xs{i}", [128, D], F32) for i in range(3)]
            xT = sb(ph, "p1xT", [128, KC, WT], F32)
            sq = sb(ph, "p1sq", [128, KC, WT], BF16)
            hT = sb(ph, "p1hT", [128, KC, WT], BF16)
            rstd = sb(ph, "p1rstd", [128, WT], F32)
            rc = [sb(ph, f"ropec{i}", [128, WT], F32) for i in range(2)]
            rs = [sb(ph, f"ropes{i}", [128, WT], F32) for i in range(2)]
            qraw = [sb(ph, f"qraw{i}", [128, WT], BF16) for i in range(2)]
            t1 = [sb(ph, f"rt1{i}", [128, WT], F32) for i in range(2)]
            t2 = [sb(ph, f"rt2{i}", [128, WT], F32) for i in range(2)]
            qo = [sb(ph, f"qo{i}", [128, WT], BF16) for i in range(4)]
            vo = [sb(ph, f"vo{i}", [128, 512], BF16) for i in range(3)]
            n_t1 = 0 if stop_after in ("c0", "c1") else (2 if (stop_after and "short" in stop_after) else S // WT)
            cnt = 0
            parts = stop_after.split(':')[1] if (stop_after and ':' in stop_after) else 'nqv123'
            if stop_after and ':' in stop_after:
                n_t1 = 2
            for i in range(n_t1):
                load_T(xs_st, lambda s, n, i=i: xb[(i * 4 + s) * 128:(i * 4 + s) * 128 + n, :], 4, [128] * 4, xT, "p1")
                if 'n' in parts:
                    rmsnorm_T(xT, WT, 0, hT, sq, rstd)
                rci, rsi = rc[i % 2], rs[i % 2]
                P.dma("sync", lambda e, rci=rci, i=i: e.dma_start(out=rci[:, :], in_=ropeC[:, i * WT:(i + 1) * WT]),
                      rci.b, writes=[rci.b])
                P.dma("sync", lambda e, rsi=rsi, i=i: e.dma_start(out=rsi[:, :], in_=ropeS[:, i * WT:(i + 1) * WT]),
                      rsi.b, writes=[rsi.b])
                for c in ([c_ for c_ in range(8) if ('q' in parts or ('a' in parts and c_ < 4) or ('b' in parts and c_ >= 4))]):
                    pb = pbank()
                    fns = [lambda e, kc=kc, c=c, pb=pb: e.matmul(
                        pb[:, :], wqk_sb[c][:, kc * 128:(kc + 1) * 128], hT[:, kc, :],
                        start=(kc == 0), stop=(kc == KC - 1)) for kc in range(KC)]
                    P.op("tensor", fns, reads=[wqk_sb[c].b, hT.b], writes=[pb.b])
                    cnt += 1
                    q_o = qo[cnt % 4]
                    if c < 4:
                        qr, a1, a2 = qraw[cnt % 2], t1[cnt % 2], t2[cnt % 2]
                        P.op("scalar", lambda e, qr=qr, pb=pb: e.activation(qr[:, :], pb[:, :], AF.Copy),
                             reads=[pb.b], writes=[qr.b])
                        pb2 = pbank()
                        if 'P' not in parts:
                            P.op("tensor", lambda e, pb2=pb2, qr=qr: e.matmul(pb2[:, :], perm_bf, qr[:, :], start=True, stop=True),
                                 reads=[qr.b, cbf.b], writes=[pb2.b])
                        if '1' in parts: P.op("vector", lambda e, a1=a1, pb=pb, rci=rci: e.scalar_tensor_tensor(a1[:, :], pb[:, :], 1.0, (rstd if 'R' in parts else rci)[:, :], ALU.mult, ALU.mult),
                             reads=[pb.b, (rstd if 'R' in parts else rci).b], writes=[a1.b])
                        if '2' in parts: P.op("vector", lambda e, a2=a2, pb2=pb2, rsi=rsi: e.scalar_tensor_tensor(a2[:, :], pb2[:, :], 1.0, rsi[:, :], ALU.mult, ALU.mult),
                             reads=[pb2.b, rsi.b], writes=[a2.b])
                        if '3' in parts: P.op("vector", lambda e, a1=a1, a2=a2, q_o=q_o: e.tensor_tensor(q_o[:, :], a1[:, :], a2[:, :], ALU.add),
                             reads=[a1.b, a2.b], writes=[q_o.b])
                    else:
                        sc = (128.0 ** -0.5) if c in (4, 6) else 1.0
                        P.op("scalar", lambda e, q_o=q_o, pb=pb, sc=sc: e.activation(q_o[:, :], pb[:, :], AF.Copy, scale=sc),
                             reads=[pb.b], writes=[q_o.b])
                    P.dma("sync", lambda e, q_o=q_o, c=c, i=i: e.dma_start(out=qkT[c, :, i * WT:(i + 1) * WT], in_=q_o[:, :]),
                          q_o.b, reads=[q_o.b], writes=[d_qk])
                for s in (range(4) if 'v' in parts else []):
                    pb = pbank()
                    fns = [lambda e, kc=kc, s=s, pb=pb: e.matmul(
                        pb[:, :], hT[:, kc, s * 128:(s + 1) * 128], wv_sb[:, kc * 512:(kc + 1) * 512],
                        start=(kc == 0), stop=(kc == KC - 1)) for kc in range(KC)]
                    P.op("tensor", fns, reads=[wv_sb.b, hT.b], writes=[pb.b])
                    cnt += 1
                    v_o = vo[cnt % 3]
                    P.op("scalar", lambda e, v_o=v_o, pb=pb: e.activation(v_o[:, :], pb[:, :], AF.Copy),
                         reads=[pb.b], writes=[v_o.b])
                    blk = i * 4 + s
                    P.dma("sync", lambda e, v_o=v_o, blk=blk: e.dma_start(
                        out=vS[:, :, blk, :].rearrange("h p d -> p h d"),
                        in_=v_o[:, :].rearrange("p (h d) -> p h d", h=4)),
                        v_o.b, reads=[v_o.b], writes=[d_v])
            P.barrier()

        short = bool(stop_after) and "short" in stop_after
        nqt = 2 if short else S // WT
        run_p2 = not (stop_after and (stop_after in ("c0", "c1", "p1", "p1short") or ":" in stop_after))
        ag_cnt = [0]
        if stop_after and stop_after.endswith("X"):
            for _ in range(20):
                P.new_sem("x")
        if run_p2:
          with ExitStack() as ph:
            KT = [sb(ph, f"KT{h}", [128, S], BF16) for h in range(2)]
            QT = [sb(ph, f"QT{h}", [128, S], BF16) for h in range(2)]
            Vh = [sb(ph, f"Vh{h}", [128, 64 * 128], BF16) for h in range(2)]

            sem_cc = P.new_sem("cc")

            def emit_ag(k):
                gp = P.engs["gpsimd"]
                P._waits(gp, P._deps([d_send[k]], [], []))
                gp.q.append(lambda e: e.collective_compute(
                    "AllGather", ALU.bypass, replica_groups=[[0, 1, 2, 3], [4, 5, 6, 7]],
                    ins=[att_send[k][:, :]], outs=[att_all[k][:, :]]).then_inc(sem_cc))
                ag_cnt[0] += 1
                d_all[k].w = [(sem_cc, ag_cnt[0])]

            def load_head(slot, cq, ck, cv):
                P.dma("sync", lambda e: e.dma_start(out=KT[slot][:, :], in_=qkT[ck]), KT[slot].b, reads=[d_qk], writes=[KT[slot].b])
                P.dma("gpsimd", lambda e: e.dma_start(out=QT[slot][:, :], in_=qkT[cq]), QT[slot].b, reads=[d_qk], writes=[QT[slot].b])
                P.dma("sync", lambda e: e.dma_start(out=Vh[slot][:, :], in_=vS[cv].rearrange("p b d -> p (b d)")),
                      Vh[slot].b, reads=[d_v], writes=[Vh[slot].b])

            Eb = [sb(ph, f"Eb{i}", [128, 2, WT], BF16) for i in range(4)]
            accD = sb(ph, "accD", [128, 2, WT], F32)
            acc_hi = sb(ph, "acc_hi", [128, 2, WT], BF16)
            acc_lo = sb(ph, "acc_lo", [128, 2, WT], BF16)
            fin = [sb(ph, f"fin{i}", [128, WT], F32) for i in range(6)]
            sq1 = sb(ph, "sq1", [128, WT], BF16)
            onb = [sb(ph, f"onb{i}", [128, WT], BF16) for i in range(2)]
            for hA in range(2):
                load_head(hA, 2 * hA, 2 * hA + 1, hA)
            Dn0, Dn1, O0, O1 = psb[4], psb[5], psb[6], psb[7]
            for hA in range(2):
                K_, Q_, V_ = KT[hA], QT[hA], Vh[hA]
                units = [(i, j) for i in range(nqt) for j in range(4 * i + 4)]

                def emit_S(u, K_=K_, Q_=Q_):
                    i, j = units[u]
                    jj = j - 4 * i
                    n0 = 128 * jj if jj > 0 else 0
                    q0 = i * WT
                    b0, b1 = psb[2 * (u % 2)], psb[2 * (u % 2) + 1]
                    P.op("tensor", [
                        lambda e: e.matmul(b0[:, n0:], K_[0:64, j * 128:(j + 1) * 128], Q_[0:64, q0 + n0:q0 + WT], start=True, stop=True),
                        lambda e: e.matmul(b1[:, n0:], K_[64:128, j * 128:(j + 1) * 128], Q_[64:128, q0 + n0:q0 + WT], start=True, stop=True)],
                        reads=[K_.b, Q_.b], writes=[b0.b, b1.b])
                    E = Eb[u % 4]
                    pair = ps_all[:, 2 * (u % 2):2 * (u % 2) + 2, n0:]
                    P.op("scalar", lambda e: e.activation(E[:, :, n0:], pair, AF.Exp, scale=0.125),
                         reads=[b0.b, b1.b], writes=[E.b])
                    if jj >= 0:
                        P.op("gpsimd", lambda e: e.affine_select(
                            out=E[:, :, n0:n0 + 128], in_=E[:, :, n0:n0 + 128], pattern=[[0, 2], [1, 128]],
                            compare_op=ALU.is_ge, fill=0.0, base=0, channel_multiplier=-1),
                            reads=[E.b], writes=[E.b])

                emit_S(0)

                def da_step(u, hA=hA, V_=V_):
                    i, j = units[u]
                    nj = 4 * i + 4
                    jj = j - 4 * i
                    n0 = 128 * jj if jj > 0 else 0
                    q0 = i * WT
                    if u + 1 < len(units):
                        emit_S(u + 1)
                    E = Eb[u % 4]
                    st, sp_ = (j == 0), (j == nj - 1)
                    P.op("tensor", [
                        lambda e, E=E, n0=n0, st=st, sp_=sp_, j=j: e.matmul(O0[:, n0:], V_[:, j * 128:(j + 1) * 128], E[:, 0, n0:], start=st, stop=sp_),
                        lambda e, E=E, n0=n0, st=st, sp_=sp_, j=j: e.matmul(O1[:, n0:], V_[:, j * 128:(j + 1) * 128], E[:, 1, n0:], start=st, stop=sp_)],
                        reads=[E.b, V_.b], writes=[O0.b, O1.b])
                    if st:
                        P.op("vector", lambda e, E=E: e.tensor_copy(accD[:, :, :], E[:, :, :]), reads=[E.b], writes=[accD.b])
                    else:
                        P.op("vector", lambda e, E=E, n0=n0: e.tensor_tensor(accD[:, :, n0:], accD[:, :, n0:], E[:, :, n0:], ALU.add),
                             reads=[E.b], writes=[accD.b])
                    if j == nj - 1:
                        r1, a1, r2, a2, o_, rs_ = fin
                        on = onb[i % 2]
                        P.op("vector", lambda e: e.tensor_copy(acc_hi[:, :, :], accD[:, :, :]), reads=[accD.b], writes=[acc_hi.b])
                        P.op("vector", lambda e: e.tensor_tensor(acc_lo[:, :, :], accD[:, :, :], acc_hi[:, :, :], ALU.subtract),
                             reads=[accD.b, acc_hi.b], writes=[acc_lo.b])
                        P.op("tensor", [
                            lambda e: e.matmul(Dn0[:, :], ones_bf, acc_hi[:, 0, :], start=True, stop=False),
                            lambda e: e.matmul(Dn0[:, :], ones_bf, acc_lo[:, 0, :], start=False, stop=True),
                            lambda e: e.matmul(Dn1[:, :], ones_bf, acc_hi[:, 1, :], start=True, stop=False),
                            lambda e: e.matmul(Dn1[:, :], ones_bf, acc_lo[:, 1, :], start=False, stop=True)],
                            reads=[acc_hi.b, acc_lo.b, cbf.b], writes=[Dn0.b, Dn1.b])
                        P.op("vector", lambda e: e.reciprocal(r1[:, :], Dn0[:, :]), reads=[Dn0.b], writes=[r1.b])
                        P.op("vector", lambda e: e.tensor_tensor(a1[:, :], O0[:, :], r1[:, :], ALU.mult), reads=[O0.b, r1.b], writes=[a1.b])
                        P.op("vector", lambda e: e.reciprocal(r2[:, :], Dn1[:, :]), reads=[Dn1.b], writes=[r2.b])
                        P.op("vector", lambda e: e.tensor_tensor(a2[:, :], O1[:, :], r2[:, :], ALU.mult), reads=[O1.b, r2.b], writes=[a2.b])
                        P.op("vector", lambda e: e.scalar_tensor_tensor(o_[:, :], a2[:, :], neglam, a1[:, :], ALU.mult, ALU.add),
                             reads=[a1.b, a2.b, lms.b], writes=[o_.b])
                        P.op("scalar", lambda e: e.activation(sq1[:, :], o_[:, :], AF.Square), reads=[o_.b], writes=[sq1.b])
                        pbk = psb[2 * ((u + 1) % 2)]
                        P.op("tensor", lambda e, pbk=pbk: e.matmul(pbk[:, :], ones_bf, sq1[:, :], start=True, stop=True),
                             reads=[sq1.b, cbf.b], writes=[pbk.b])
                        P.op("scalar", lambda e, pbk=pbk: e.activation(rs_[:, :], pbk[:, :], AF.Sqrt, bias=epsc[:, 0:1], scale=1.0 / 128),
                             reads=[pbk.b, epsc.b], writes=[rs_.b])
                        P.op("vector", lambda e: e.reciprocal(rs_[:, :], rs_[:, :]), reads=[rs_.b], writes=[rs_.b])
                        P.op("vector", lambda e, on=on: e.scalar_tensor_tensor(on[:, :], o_[:, :], gsub8, rs_[:, :], ALU.mult, ALU.mult),
                             reads=[o_.b, rs_.b, lms.b], writes=[on.b])
                        P.dma("sync", lambda e, on=on, hA=hA, i=i: e.dma_start(
                            out=att_send_bf[i // 2][hA * 128:(hA + 1) * 128, (i % 2) * WT:(i % 2 + 1) * WT], in_=on[:, :]),
                            on.b, reads=[on.b], writes=[d_send[i // 2]])

                for u in range(len(units)):
                    da_step(u)

            e32 = [sb(ph, f"e32_{i}", [128, 2, WT], F32) for i in range(3)]
            spb = [sb(ph, f"spb_{i}", [128, 2, WT], BF16) for i in range(3)]
            exb = [sb(ph, f"exb_{i}", [128, 2, WT], F32) for i in range(3)]
            abf = [sb(ph, f"abf_{i}", [128, 2, WT], BF16) for i in range(3)]
            obf = sb(ph, "obf", [128, 2, WT], BF16)
            zer = sb(ph, "zer", [128, 128], BF16)
            P.op("vector", lambda e: e.memset(zer[:, :], 0.0), writes=[zer.b])
            for h in range(2):
                load_head(h, 4 + 2 * h, 5 + 2 * h, 2 + h)
            units = [(i, j) for i in range(nqt) for j in reversed(range(4 * i + 4))]

            def geom(u):
                i, j = units[u]
                jj = j - 4 * i
                n0 = 128 * jj if jj > 0 else 0
                return i, j, jj, n0, i * WT

            def emit_Z(u):
                i, j, jj, n0, q0 = geom(u)
                k = u % 2
                P.op("tensor", [lambda e, h=h: e.matmul(psb[2 * k + h][:, n0:], KT[h][:, j * 128:(j + 1) * 128],
                                                        QT[h][:, q0 + n0:q0 + WT], start=True, stop=True) for h in range(2)],
                     reads=[KT[0].b, QT[0].b, KT[1].b, QT[1].b], writes=[psb[2 * k].b, psb[2 * k + 1].b])

            def emit_exp(u):
                i, j, jj, n0, q0 = geom(u)
                k = u % 2
                e_ = e32[u % 3]
                P.op("scalar", lambda e: e.activation(e_[:, :, n0:], ps_all[:, 2 * k:2 * k + 2, n0:], AF.Exp),
                     reads=[psb[2 * k].b, psb[2 * k + 1].b], writes=[e_.b])

            def emit_ln(u):
                i, j, jj, n0, q0 = geom(u)
                e_, s_ = e32[u % 3], spb[u % 3]
                P.op("scalar", lambda e: e.activation(s_[:, :, n0:], e_[:, :, n0:], AF.Ln, bias=onec[:, 0:1]),
                     reads=[e_.b, onec.b], writes=[s_.b])
                if jj >= 0:
                    for t_ in (s_, e_):
                        P.op("gpsimd", lambda e, t_=t_: e.affine_select(
                            out=t_[:, :, n0:n0 + 128], in_=t_[:, :, n0:n0 + 128], pattern=[[0, 2], [1, 128]],
                            compare_op=ALU.is_ge, fill=0.0, base=-1, channel_multiplier=-1),
                            reads=[t_.b], writes=[t_.b])

            emit_Z(0)
            emit_exp(0)
            emit_ln(0)
            if len(units) > 1:
                emit_Z(1)

            def sb_step(u):
                i, j, jj, n0, q0 = geom(u)
                first = (j == 4 * i + 3)
                last = (j == 0)
                s_, e_, x_, a_ = spb[u % 3], e32[u % 3], exb[u % 3], abf[u % 3]
                R0, R1, O0_, O1_ = psb[4], psb[5], psb[6], psb[7]
                if first:
                    P.op("tensor", [
                        lambda e: e.matmul(R0[:, :], zer[:, :], QT[0][:, q0:q0 + WT], start=True, stop=True),
                        lambda e: e.matmul(R1[:, :], zer[:, :], QT[1][:, q0:q0 + WT], start=True, stop=True),
                        lambda e: e.matmul(O0_[:, :], zer[:, :], QT[0][:, q0:q0 + WT], start=True, stop=True),
                        lambda e: e.matmul(O1_[:, :], zer[:, :], QT[1][:, q0:q0 + WT], start=True, stop=True)],
                        reads=[zer.b, QT[0].b, QT[1].b], writes=[R0.b, R1.b, O0_.b, O1_.b])
                P.op("tensor", [
                    lambda e: e.matmul(R0[:, n0:], tinc_bf, s_[:, 0, n0:], start=False, stop=True),
                    lambda e: e.matmul(R1[:, n0:], tinc_bf, s_[:, 1, n0:], start=False, stop=True)],
                    reads=[s_.b, cbf.b], writes=[R0.b, R1.b])
                if u + 1 < len(units):
                    emit_exp(u + 1)
                P.op("scalar", lambda e: e.activation(x_[:, :, n0:], ps_all[:, 4:6, n0:], AF.Exp), reads=[R0.b, R1.b], writes=[x_.b])
                if u + 1 < len(units):
                    emit_ln(u + 1)
                P.op("tensor", [
                    lambda e: e.matmul(R0[:, n0:], tcomp_bf, s_[:, 0, n0:], start=False, stop=True),
                    lambda e: e.matmul(R1[:, n0:], tcomp_bf, s_[:, 1, n0:], start=False, stop=True)],
                    reads=[s_.b, cbf.b], writes=[R0.b, R1.b])
                P.op("vector", lambda e: e.tensor_tensor(a_[:, :, n0:], e_[:, :, n0:], x_[:, :, n0:], ALU.mult),
                     reads=[e_.b, x_.b], writes=[a_.b])
                P.op("tensor", [
                    lambda e: e.matmul(O0_[:, n0:], Vh[0][:, j * 128:(j + 1) * 128], a_[:, 0, n0:], start=False, stop=True),
                    lambda e: e.matmul(O1_[:, n0:], Vh[1][:, j * 128:(j + 1) * 128], a_[:, 1, n0:], start=False, stop=True)],
                    reads=[a_.b, Vh[0].b, Vh[1].b], writes=[O0_.b, O1_.b])
                if u + 2 < len(units):
                    emit_Z(u + 2)
                if last:
                    P.op("scalar", lambda e: e.activation(obf[:, :, :], ps_all[:, 6:8, :], AF.Copy), reads=[O0_.b, O1_.b], writes=[obf.b])
                    P.dma("sync", [lambda e, h=h: e.dma_start(
                        out=att_send_bf[i // 2][256 + h * 128:256 + (h + 1) * 128, (i % 2) * WT:(i % 2 + 1) * WT], in_=obf[:, h, :])
                        for h in range(2)], obf.b, reads=[obf.b], writes=[d_send[i // 2]])
                    if i % 2 == 1:
                        emit_ag(i // 2)

            for u in range(len(units)):
                sb_step(u)
            P.barrier()

        if debug and run_p2:
            tmpa0 = sb(root, "tmpa0", [128, 512], F32)
            for k in range(4):
                P.dma("sync", lambda e, k=k: e.dma_start(out=tmpa0[:, :], in_=att_all[0][k * 128:(k + 1) * 128, :]), tmpa0.b,
                      reads=[d_all[0]], writes=[tmpa0.b])
                P.dma("sync", lambda e, k=k: e.dma_start(out=dbg["att"][k * 128:(k + 1) * 128, :], in_=tmpa0[:, :]), tmpa0.b,
                      reads=[tmpa0.b], writes=[d_dbg])
        run_p3 = (stop_after is None) or stop_after.startswith("p3")
        if run_p3:
          with ExitStack() as ph:
            xs1 = [sb(ph, "p3xs", [128, D], F32)]
            xT3 = sb(ph, "p3xT", [128, KC, WT], F32)
            xT3.bs = [Buf(f"xT3{k}") for k in range(KC)]
            sq3 = sb(ph, "p3sq", [128, KC, WT], BF16)
            hT3 = sb(ph, "p3hT", [128, KC, WT], BF16)
            rstd3 = sb(ph, "p3rstd", [128, WT], F32)
            wsm = [sb(ph, f"wsm{i}", [128, KC * 128], BF16) for i in range(8)]
            cbuf = sb(ph, "cbuf", [128, 2 * NFC, 2], F32)
            Kmem = sb(ph, "Kmem", [128, 4, NMEM], BF16)
            Vmem = sb(ph, "Vmem", [128, 2, 512], BF16)
            P.op("vector", lambda e: e.memset(cbuf[:, :, :], 0.0), writes=[cbuf.b])

            def stream(items, slots, pf):
                loaded = []

                def ensure(k):
                    while len(loaded) <= min(k, len(items) - 1):
                        ap_, n_ = items[len(loaded)]
                        loaded.append(load_w(slots, ap_, [n_]))
                for k in range(len(items)):
                    ensure(k + pf)
                    yield loaded[k]

            def mm_group(pb, W, w, rhs_fn, nk, wbufs, rbufs):
                fns = [lambda e, kc=kc: e.matmul(pb[:, 0:W], w[:, kc * 128:(kc + 1) * 128], rhs_fn(kc),
                                                 start=(kc == 0), stop=(kc == nk - 1)) for kc in range(nk)]
                P.op("tensor", fns, reads=[w.b] + list(rbufs), writes=[pb.b])

            def resid_add(m, W, pb):
                P.op("vector", lambda e: e.tensor_tensor(xT3[:, m, 0:W], pb[:, 0:W], xT3[:, m, 0:W], ALU.add),
                     reads=[pb.b], writes=[xT3.bs[m]])

            def norm3(W, gcol0, outT):
                xT3.b.w = [t for b_ in xT3.bs for t in b_.w]
                rmsnorm_T(xT3, W, gcol0, outT, sq3, rstd3)
                for b_ in xT3.bs:
                    b_.r = b_.r + xT3.b.r
                xT3.b.r = []

            with ExitStack() as pm:
                mT = sb(pm, "mT", [128, KC, NMEM], F32)
                msq = sb(pm, "msq", [128, KC, NMEM], BF16)
                mnT = sb(pm, "mnT", [128, KC, NMEM], BF16)
                mr = sb(pm, "mr", [128, NMEM], F32)
                wxv_sb = sb(pm, "wxv_sb", [128, KC * 512], BF16)
                P.dma("gpsimd", lambda e: e.dma_start(out=wxv_sb[:, :], in_=wxv[:, :, :].rearrange("p k j -> p (k j)")),
                      wxv_sb.b, writes=[wxv_sb.b])
                load_T(xs1, lambda s_, n: memb[s_ * 128:s_ * 128 + n, :], 2, [128, 128], mT, "mem")
                rmsnorm_T(mT, NMEM, 2 * KC, mnT, msq, mr)
                for h, w in enumerate(stream([(wxk[h].rearrange("p k j -> p (k j)"), KC * 128) for h in range(4)], wsm, 3)):
                    pb = pbank()
                    mm_group(pb, NMEM, w, lambda kc: mnT[:, kc, :], KC, None, [mnT.b])
                    P.op("scalar", lambda e, pb=pb, h=h: e.activation(Kmem[:, h, :], pb[:, 0:NMEM], AF.Copy),
                         reads=[pb.b], writes=[Kmem.b])
                for blk in range(2):
                    pb = pbank()
                    fns = [lambda e, kc=kc, blk=blk, pb=pb: e.matmul(
                        pb[:, :], mnT[:, kc, blk * 128:(blk + 1) * 128], wxv_sb[:, kc * 512:(kc + 1) * 512],
                        start=(kc == 0), stop=(kc == KC - 1)) for kc in range(KC)]
                    P.op("tensor", fns, reads=[mnT.b, wxv_sb.b], writes=[pb.b])
                    P.op("scalar", lambda e, pb=pb, blk=blk: e.activation(Vmem[:, blk, :], pb[:, :], AF.Copy),
                         reads=[pb.b], writes=[Vmem.b])
                P.barrier()

            def dump_x(stage, src=None):
                if not debug:
                    return
                src = src or xT3
                rb = getattr(src, "bs", None) or [src.b]
                P.dma("gpsimd", lambda e: e.dma_start(out=dbg["x"][stage], in_=src[:, :, :].rearrange("p k w -> p (k w)")),
                      rb[0], reads=rb, writes=[d_dbg])

            def a_oa(h):
                return 4 * (h // 2) + (h % 2)

            def a_ob(h):
                return 4 * (h // 2) + 2 + (h % 2)

            def tile(W, row0, halo, t):
                nsub = (W + 127) // 128
                sizes = [min(128, W - 128 * k) for k in range(nsub)]
                xT3.b.w = []
                xT3.b.r = [t_ for b_ in xT3.bs for t_ in (b_.w + b_.r)]
                load_T(xs1, lambda s_, n: xo[row0 + s_ * 128:row0 + s_ * 128 + n, :], nsub, sizes, xT3, "p3")
                for b_ in xT3.bs:
                    b_.w = list(xT3.b.w)
                    b_.r = []
                if not halo and t == 0:
                    dump_x(0)
                norm3(W, 0, hT3)
                with ExitStack() as sc:
                    cand = [sb(sc, f"cand{q}", [128, 4, WT], BF16) for q in range(4)]
                    asel = sb(sc, "asel", [128, KC, WT], BF16)
                    asel.bs = [Buf(f"asel{k}") for k in range(4)]
                    merged = sb(sc, "merged", [128, KC, WT], BF16)
                    merged.bs = [Buf(f"mg{k}") for k in range(KC)]
                    sg = [sb(sc, f"sg{i}", [128, WT], F32) for i in range(4)]
                    tmpm = [sb(sc, f"tmpm{i}", [128, WT], F32) for i in range(2)]
                    qs = [1, 2, 3] if halo else [0, 1, 2, 3]
                    if short:
                        qs = [] if (halo or "nocand" in stop_after) else [0]
                        if halo or "nocand" in stop_after:
                            for qtr in range(4):
                                P.op("vector", lambda e, qtr=qtr: e.memset(asel[:, qtr * 4:(qtr + 1) * 4, 0:W], 0.0), writes=[asel.bs[qtr]])
                    for qtr in range(4):
                        for q in qs:
                            chunk, coff = (2 * q - 1, 1024 - HALO) if halo else (2 * q + t // 2, (t % 2) * WT)
                            src = att_all_bf[chunk][qtr * 512:(qtr + 1) * 512, coff:coff + W].rearrange("(a p) w -> p a w", p=128)
                            P.dma("sync", lambda e, q=q, src=src: e.dma_start(out=cand[q][:, :, 0:W], in_=src),
                                  cand[q].b, reads=[d_all[chunk]], writes=[cand[q].b])
                        dst = asel[:, qtr * 4:(qtr + 1) * 4, 0:W]
                        for n_, q in enumerate(qs):
                            if n_ == 0:
                                P.op("vector", lambda e, q=q, dst=dst: e.tensor_scalar(
                                    dst, cand[q][:, :, 0:W], mk[:, q:q + 1], None, ALU.mult),
                                    reads=[cand[q].b, mk.b], writes=[asel.bs[qtr]])
                            else:
                                P.op("vector", lambda e, q=q, dst=dst: e.scalar_tensor_tensor(
                                    dst, cand[q][:, :, 0:W], mk[:, q:q + 1], dst, ALU.mult, ALU.add),
                                    reads=[cand[q].b, mk.b], writes=[asel.bs[qtr]])
                    items = []
                    for m in range(KC):
                        items += [(wg[m].rearrange("p k j -> p (k j)"), KC * 128), (wg[KC + m].rearrange("p k j -> p (k j)"), KC * 128),
                                  (wpa[m].rearrange("p k j -> p (k j)"), 8 * 128), (wpb[m].rearrange("p k j -> p (k j)"), 8 * 128)]
                    ws = stream(items, wsm, 4)
                    for m in range(KC):
                        wga, wgb, wa, wb_ = next(ws), next(ws), next(ws), next(ws)
                        A, B, C, Dd = pbank(), pbank(), pbank(), pbank()
                        mm_group(A, W, wga, lambda kc: hT3[:, kc, 0:W], KC, None, [hT3.b])
                        mm_group(B, W, wgb, lambda kc: hT3[:, kc, 0:W], KC, None, [hT3.b])
                        mm_group(C, W, wa, lambda kc: asel[:, a_oa(kc), 0:W], 8, None, asel.bs)
                        mm_group(Dd, W, wb_, lambda kc: asel[:, a_ob(kc), 0:W], 8, None, asel.bs)
                        sga, sgb = sg[(2 * m) % 4], sg[(2 * m + 1) % 4]
                        P.op("scalar", lambda e, sga=sga, A=A: e.activation(sga[:, 0:W], A[:, 0:W], AF.Sigmoid), reads=[A.b], writes=[sga.b])
                        P.op("scalar", lambda e, sgb=sgb, B=B: e.activation(sgb[:, 0:W], B[:, 0:W], AF.Sigmoid), reads=[B.b], writes=[sgb.b])
                        t0_, t1_ = tmpm
                        P.op("vector", lambda e, sga=sga, C=C: e.tensor_tensor(t0_[:, 0:W], C[:, 0:W], sga[:, 0:W], ALU.mult),
                             reads=[sga.b, C.b], writes=[t0_.b])
                        P.op("vector", lambda e, sgb=sgb, Dd=Dd: e.tensor_tensor(t1_[:, 0:W], Dd[:, 0:W], sgb[:, 0:W], ALU.mult),
                             reads=[sgb.b, Dd.b], writes=[t1_.b])
                        P.op("vector", lambda e, m=m: e.tensor_tensor(merged[:, m, 0:W], t0_[:, 0:W], t1_[:, 0:W], ALU.add),
                             reads=[t0_.b, t1_.b], writes=[merged.bs[m]])
                    for m, w in enumerate(stream([(wout[m].rearrange("p k j -> p (k j)"), KC * 128) for m in range(KC)], wsm, 4)):
                        pb = pbank()
                        mm_group(pb, W, w, lambda kc: merged[:, kc, 0:W], KC, None, merged.bs)
                        resid_add(m, W, pb)
                    if not halo and t == 0:
                        dump_x(1)
                        if stop_after in ("p3mixshort", "p3mixnocandshort"):
                            dump_x(2, hT3)
                            P.dma("gpsimd", lambda e: e.dma_start(out=dbg["x"][0][:, 0:4 * WT], in_=cand[0][:, :, :].rearrange("p k w -> p (k w)")),
                                  cand[0].b, reads=[cand[0].b], writes=[d_dbg])
                            P.dma("gpsimd", lambda e: e.dma_start(out=dbg["x"][0][:, 4 * WT:4 * WT + 8], in_=mk[:, :]),
                                  mk.b, reads=[mk.b], writes=[d_dbg])
                            dump_x(3, asel)
                            dump_x(4, merged)
                    P.barrier()
                if stop_after in ("p3mixshort", "p3mixnocandshort"):
                    return
                norm3(W, KC, hT3)
                with ExitStack() as sc:
                    qx = sb(sc, "qx", [128, 4, WT], BF16)
                    Ex = [sb(sc, f"Ex{i}", [128, 2, WT], BF16) for i in range(2)]
                    ox = sb(sc, "ox", [128, 4, WT], BF16)
                    ox.bs = [Buf(f"ox{k}") for k in range(4)]
                    rx = [sb(sc, f"rx{i}", [128, WT], F32) for i in range(2)]
                    for h, w in enumerate(stream([(wxq[h].rearrange("p k j -> p (k j)"), KC * 128) for h in range(4)], wsm, 4)):
                        pb = pbank()
                        mm_group(pb, W, w, lambda kc: hT3[:, kc, 0:W], KC, None, [hT3.b])
                        P.op("scalar", lambda e, pb=pb, h=h: e.activation(qx[:, h, 0:W], pb[:, 0:W], AF.Copy), reads=[pb.b], writes=[qx.b])
                    for h in range(4):
                        S0, S1, Dn, O = pbank(), pbank(), pbank(), pbank()
                        E = Ex[h % 2]
                        P.op("tensor", [lambda e, h=h, S0=S0: e.matmul(S0[:, 0:W], Kmem[:, h, 0:128], qx[:, h, 0:W], start=True, stop=True),
                                        lambda e, h=h, S1=S1: e.matmul(S1[:, 0:W], Kmem[:, h, 128:256], qx[:, h, 0:W], start=True, stop=True)],
                             reads=[Kmem.b, qx.b], writes=[S0.b, S1.b])
                        P.op("scalar", lambda e, E=E, S0=S0: e.activation(E[:, 0, 0:W], S0[:, 0:W], AF.Exp, scale=128.0 ** -0.5),
                             reads=[S0.b], writes=[E.b])
                        P.op("scalar", lambda e, E=E, S1=S1: e.activation(E[:, 1, 0:W], S1[:, 0:W], AF.Exp, scale=128.0 ** -0.5),
                             reads=[S1.b], writes=[E.b])
                        P.op("tensor", [lambda e, E=E, Dn=Dn: e.matmul(Dn[:, 0:W], ones_bf, E[:, 0, 0:W], start=True, stop=False),
                                        lambda e, E=E, Dn=Dn: e.matmul(Dn[:, 0:W], ones_bf, E[:, 1, 0:W], start=False, stop=True),
                                        lambda e, E=E, O=O, h=h: e.matmul(O[:, 0:W], Vmem[:, 0, h * 128:(h + 1) * 128], E[:, 0, 0:W], start=True, stop=False),
                                        lambda e, E=E, O=O, h=h: e.matmul(O[:, 0:W], Vmem[:, 1, h * 128:(h + 1) * 128], E[:, 1, 0:W], start=False, stop=True)],
                             reads=[E.b, Vmem.b, cbf.b], writes=[Dn.b, O.b])
                        r_ = rx[h % 2]
                        P.op("vector", lambda e, r_=r_, Dn=Dn: e.reciprocal(r_[:, 0:W], Dn[:, 0:W]), reads=[Dn.b], writes=[r_.b])
                        P.op("vector", lambda e, r_=r_, O=O, h=h: e.tensor_tensor(ox[:, h, 0:W], O[:, 0:W], r_[:, 0:W], ALU.mult),
                             reads=[O.b, r_.b], writes=[ox.bs[h]])
                    for m, w in enumerate(stream([(wxo[m].rearrange("p k j -> p (k j)"), 4 * 128) for m in range(KC)], wsm, 4)):
                        pb = pbank()
                        mm_group(pb, W, w, lambda kc: ox[:, kc, 0:W], 4, None, ox.bs)
                        resid_add(m, W, pb)
                    if not halo and t == 0:
                        dump_x(2)
                    P.barrier()
                norm3(W, 3 * KC, hT3)
                with ExitStack() as sc:
                    act = sb(sc, "act", [128, NFC, WT], BF16)
                    act.bs = [Buf(f"act{k}") for k in range(NFC)]
                    ub = [sb(sc, f"ub{i}", [128, WT + 2], F32) for i in range(3)]
                    yb = [sb(sc, f"yb{i}", [128, WT], F32) for i in range(4)]
                    sgf = [sb(sc, f"sgf{i}", [128, WT], F32) for i in range(2)]
                    wbg = [sb(sc, f"wbg{i}", [128, NFC * 128], BF16) for i in range(2)]
                    items = []
                    for c in range(NFC):
                        items += [(wup[c].rearrange("p k j -> p (k j)"), KC * 128), (wup[NFC + c].rearrange("p k j -> p (k j)"), KC * 128)]
                    ws = stream(items, wsm, 5)
                    cnt = 0
                    for c in range(NFC):
                        ys = []
                        for half in range(2):
                            uc = half * NFC + c
                            w = next(ws)
                            pb = pbank()
                            mm_group(pb, W, w, lambda kc: hT3[:, kc, 0:W], KC, None, [hT3.b])
                            cnt += 1
                            u_, y_ = ub[cnt % 3], yb[cnt % 4]
                            ys.append(y_)
                            P.op("vector", lambda e, u_=u_, uc=uc: e.tensor_copy(u_[:, 0:2], cbuf[:, uc, :]), reads=[cbuf.b], writes=[u_.b])
                            P.op("scalar", lambda e, u_=u_, pb=pb: e.activation(u_[:, 2:W + 2], pb[:, 0:W], AF.Copy), reads=[pb.b], writes=[u_.b])
                            P.op("vector", lambda e, u_=u_, uc=uc: e.tensor_copy(cbuf[:, uc, :], u_[:, W:W + 2]), reads=[u_.b], writes=[cbuf.b])
                            if not halo:
                                P.op("scalar", lambda e, y_=y_, pb=pb, uc=uc: e.activation(
                                    y_[:, 0:W], pb[:, 0:W], AF.Identity, bias=cb[:, uc:uc + 1], scale=cw[:, uc * 3 + 2:uc * 3 + 3]),
                                    reads=[pb.b, cb.b, cw.b], writes=[y_.b])
                                P.op("vector", lambda e, y_=y_, u_=u_, uc=uc: e.scalar_tensor_tensor(
                                    y_[:, 0:W], u_[:, 1:W + 1], cw[:, uc * 3 + 1:uc * 3 + 2], y_[:, 0:W], ALU.mult, ALU.add),
                                    reads=[u_.b, cw.b], writes=[y_.b])
                                P.op("vector", lambda e, y_=y_, u_=u_, uc=uc: e.scalar_tensor_tensor(
                                    y_[:, 0:W], u_[:, 0:W], cw[:, uc * 3:uc * 3 + 1], y_[:, 0:W], ALU.mult, ALU.add),
                                    reads=[u_.b, cw.b], writes=[y_.b])
                        if not halo:
                            s_ = sgf[c % 2]
                            P.op("scalar", lambda e, s_=s_, yg=ys[0]: e.activation(s_[:, 0:W], yg[:, 0:W], AF.Silu), reads=[ys[0].b], writes=[s_.b])
                            P.op("vector", lambda e, s_=s_, yv=ys[1], c=c: e.tensor_tensor(act[:, c, 0:W], s_[:, 0:W], yv[:, 0:W], ALU.mult),
                                 reads=[s_.b, ys[1].b], writes=[act.bs[c]])
                    if halo:
                        P.op("vector", lambda e: e.tensor_scalar(cbuf[:, :, :], cbuf[:, :, :], mk[:, 4:5], None, ALU.mult),
                             reads=[mk.b], writes=[cbuf.b])
                    else:
                        for m, w in enumerate(stream([(wdn[m].rearrange("p k j -> p (k j)"), NFC * 128) for m in range(KC)], wbg, 1)):
                            pb = pbank()
                            mm_group(pb, W, w, lambda kc: act[:, kc, 0:W], NFC, None, act.bs)
                            resid_add(m, W, pb)
                    if not halo and t == 0:
                        dump_x(3)
                    P.barrier()
                if halo:
                    return
                with ExitStack() as sc:
                    yT = sb(sc, "yT", [128, KC, WT], F32)
                    norm3(W, 4 * KC, yT)
                    if t == 0:
                        dump_x(4, yT)
                    xs = xs1[0]
                    for s_ in range(4):
                        for g in range(4):
                            pb = pbank()
                            fns = [lambda e, pb=pb, j=j, g=g, s_=s_: e.transpose(
                                pb[:, j * 128:(j + 1) * 128], yT[:, 4 * g + j, s_ * 128:(s_ + 1) * 128], ident32) for j in range(4)]
                            P.op("tensor", fns, reads=[yT.b, c32.b], writes=[pb.b])
                            if g % 2 == 0:
                                P.op("vector", lambda e, pb=pb, g=g: e.tensor_copy(xs[:, g * 512:(g + 1) * 512], pb[:, :]),
                                     reads=[pb.b], writes=[xs.b])
                            else:
                                P.op("scalar", lambda e, pb=pb, g=g: e.activation(xs[:, g * 512:(g + 1) * 512], pb[:, :], AF.Copy),
                                     reads=[pb.b], writes=[xs.b])
                        r0 = t * WT + s_ * 128
                        P.dma("sync", lambda e, r0=r0: e.dma_start(out=y[r0:r0 + 128, :], in_=xs[:, :]), xs.b, reads=[xs.b], writes=[d_y])
                    P.barrier()

            tile(HALO, 0, True, 0)
            ntile3 = 1 if (stop_after and ("short" in stop_after or "one" in stop_after)) else NTILE
            for t in range(ntile3):
                tile(WT, HALO + t * WT, False, t)
            P.barrier()

        if debug and stop_after is not None and stop_after != "c0":
            tmpd = sb(root, "tmpd", [128, S], BF16)
            for c in range(8):
                P.dma("sync", lambda e, c=c: e.dma_start(out=tmpd[:, :], in_=qkT[c]), tmpd.b, reads=[d_qk], writes=[tmpd.b])
                P.dma("sync", lambda e, c=c: e.dma_start(out=dbg["qk"][c], in_=tmpd[:, :]), tmpd.b, reads=[tmpd.b], writes=[d_dbg])
            for h in range(4):
                P.dma("sync", lambda e, h=h: e.dma_start(out=tmpd[:, :], in_=vS[h].rearrange("p b d -> p (b d)")),
                      tmpd.b, reads=[d_v], writes=[tmpd.b])
                P.dma("sync", lambda e, h=h: e.dma_start(out=dbg["v"][h].rearrange("p b d -> p (b d)"), in_=tmpd[:, :]),
                      tmpd.b, reads=[tmpd.b], writes=[d_dbg])

        if debug and run_p2:
            tmpa = sb(root, "tmpa", [128, 512], F32)
            for k in range(4):
                P.dma("sync", lambda e, k=k: e.dma_start(out=tmpa[:, :], in_=att_all[0][k * 128:(k + 1) * 128, :]), tmpa.b,
                      reads=[d_all[0]], writes=[tmpa.b])
                P.dma("sync", lambda e, k=k: e.dma_start(out=dbg["x"][4][:, k * 512:(k + 1) * 512], in_=tmpa[:, :]), tmpa.b,
                      reads=[tmpa.b], writes=[d_dbg])
        with nc.Block() as block:
            P.finish(block)
        print("instructions:", P.ninst, "sems:", P.nsem)
    return nc


def _blk(W, cols=None):
    if cols is not None:
        W = W[:, cols]
    K, N = W.shape
    return np.ascontiguousarray(W.reshape(K // 128, 128, N // 128, 128).transpose(2, 1, 0, 3))


def _rhs(W, cols=None):
    if cols is not None:
        W = W[:, cols]
    K, N = W.shape
    return np.ascontiguousarray(W.reshape(K // 128, 128, N).transpose(1, 0, 2))


def _cols(g):
    return np.ascontiguousarray(np.asarray(g, np.float32).reshape(-1, 128).T)


def _rope_tables():
    inv = (np.float32(500000.0) ** (-np.arange(0, 16, 2, dtype=np.float32) / np.float32(16))).astype(np.float32)
    ang = (np.arange(S, dtype=np.float32)[:, None] * inv[None, :]).astype(np.float32)
    cos = np.cos(ang).astype(np.float32).T
    sin = np.sin(ang).astype(np.float32).T
    C = np.ones((128, S), np.float32)
    Sg = np.zeros((128, S), np.float32)
    for base in (0, 64):
        C[base:base + 8] = cos
        C[base + 8:base + 16] = cos
        Sg[base:base + 8] = -sin
        Sg[base + 8:base + 16] = sin
    return C, Sg


def _consts():
    ident = np.eye(128, dtype=np.float32)
    perm = np.zeros((128, 128), np.float32)
    for base in (0, 64):
        for i in range(8):
            perm[base + 8 + i, base + i] = 1.0
            perm[base + i, base + 8 + i] = 1.0
    j = np.arange(128)[:, None]
    s = np.arange(128)[None, :]
    tinc = np.where(j >= s, -1.0, 0.0).astype(np.float32)
    tcomp = np.where(j < s, -1.0, 0.0).astype(np.float32)
    ones = np.ones((128, 128), np.float32)
    return np.concatenate([ident, perm, tinc, tcomp, ones], axis=1)


def prep_inputs(inp):
    f = lambda k: np.asarray(inp[k], np.float32)
    x, mem = f("x"), f("mem")
    w_in = f("w_in")[0]
    ropeC, ropeS = _rope_tables()
    consts = _consts()
    gains = np.concatenate([_cols(f("norm_mix_g")[0]), _cols(f("norm_x_g")[0]), _cols(f("norm_mem_g")[0]),
                            _cols(f("norm_ffn_g")[0]), _cols(f("final_norm_g")), _cols(f("da_subln_g")[0])], axis=1)
    cwt = f("conv_w")[0]
    convw = np.ascontiguousarray(cwt.T.reshape(2 * NFC, 128, 3).transpose(1, 0, 2)).reshape(128, 2 * NFC * 3)
    convb = _cols(f("conv_b")[0])
    lam4 = np.concatenate([f("lambda_q1")[0], f("lambda_k1")[0], f("lambda_q2")[0], f("lambda_k2")[0]])
    lam4 = np.ascontiguousarray(np.broadcast_to(lam4[None, :], (128, 256)))
    shared = dict(
        wg=_blk(w_in[:, 6144:10240]),
        wpa=_blk(f("w_proj_a")[0]), wpb=_blk(f("w_proj_b")[0]), wout=_blk(f("w_out")[0]),
        wxq=_blk(f("w_xq")[0]), wxk=_blk(f("w_xkv")[0][:, :512]), wxv=_rhs(f("w_xkv")[0][:, 512:]),
        wxo=_blk(f("w_xo")[0]), wup=_blk(f("w_up")[0]), wdn=_blk(f("w_down")[0]),
        gains=np.ascontiguousarray(gains), convw=convw, convb=convb, lam4=lam4,
        ropeC=ropeC, ropeS=ropeS, consts=consts,
    )
    maps = []
    for c in range(8):
        b, r = c // 4, c % 4
        h0, h1 = 2 * r, 2 * r + 1
        def hc(base, h):
            return np.arange(base + h * 128, base + (h + 1) * 128)
        qk_cols = np.concatenate([hc(0, h0), hc(1024, h0), hc(0, h1), hc(1024, h1),
                                  hc(3072, h0), hc(4096, h0), hc(3072, h1), hc(4096, h1)])
        v_cols = np.concatenate([hc(2048, h0), hc(2048, h1), hc(5120, h0), hc(5120, h1)])
        xo = np.zeros((HALO + NTILE * WT, D), np.float32)
        lo = 2048 * r - HALO
        if r == 0:
            xo[HALO:] = x[b, 0:2048]
        else:
            xo[:] = x[b, lo:lo + HALO + 2048]
        masks = np.zeros((128, 8), np.float32)
        masks[:, r] = 1.0
        masks[:, 4] = 0.0 if r == 0 else 1.0
        m = dict(shared)
        m.update(xb=np.ascontiguousarray(x[b]), xo=xo, memb=np.ascontiguousarray(mem[b]),
                 wqk=_blk(w_in, qk_cols), wv=_rhs(w_in, v_cols), masks=masks)
        maps.append(m)
    return maps


_NC_CACHE = {}


def kernel(**inputs):
    maps = prep_inputs(inputs)
    if "nc" not in _NC_CACHE:
        _NC_CACHE["nc"] = build()
    nc = _NC_CACHE["nc"]
    res = run_bass_kernel_spmd(nc, maps, core_ids=list(range(8)))
    out = np.zeros((2, S, D), np.float32)
    for c in range(8):
        b, r = c // 4, c % 4
        out[b, 2048 * r:2048 * (r + 1)] = np.asarray(res.results[c]["y"], np.float32)
    return out
```
